# Optimizing a Trainium2 kernel written in Bass

```python
import math
import jax, jax.numpy as jnp
from jax import lax
import numpy as np

D_MODEL = 1024
BATCH = 16
SEQ = 4096
DEPTH = 2
DEC_BATCH = 8
DEC_SEQ = 4096
PAST_LEN = 128

N_MIXERS = 2
EXPAND = 2
D_INNER = EXPAND * D_MODEL
N_HEADS = 8
DA_HEAD_DIM = D_INNER // (2 * N_HEADS)
DA_V_DIM = 2 * DA_HEAD_DIM
DA_QK_WIDTH = 2 * N_HEADS * DA_HEAD_DIM
DA_IN = 2 * DA_QK_WIDTH + 2 * D_INNER
ROPE_THETA = 10000.0
Q_BLOCK = 128
RET_QK_DIM = D_MODEL // N_HEADS
RET_V_DIM = D_INNER // N_HEADS
RET_QK_WIDTH = N_HEADS * RET_QK_DIM
RET_IN = 2 * RET_QK_WIDTH + 2 * D_INNER
RET_CHUNK = 128
RET_ROT_BASE = 10000.0
N_ATTN_LAYERS = (DEPTH + 1) // 2
N_RET_LAYERS = DEPTH // 2
NORM_EPS = 1e-6
SUBLN_EPS = 1e-5

kernel_name = "diffattn_retnet_bidir_encoder"


def rms_norm(x, g, eps):
    xf = x.astype(jnp.float32)
    y = xf * lax.rsqrt(jnp.mean(xf * xf, axis=-1, keepdims=True) + eps)
    return (y * g.astype(jnp.float32)).astype(x.dtype)


def rotary_half(x):
    S, d = x.shape[1], x.shape[-1]
    inv = ROPE_THETA ** (-jnp.arange(0, d, 2, dtype=jnp.float32) / d)
    ang = jnp.arange(S, dtype=jnp.float32)[:, None] * inv[None, :]
    cos = jnp.cos(ang)[None, :, None, :].astype(x.dtype)
    sin = jnp.sin(ang)[None, :, None, :].astype(x.dtype)
    x1, x2 = x[..., : d // 2], x[..., d // 2:]
    return jnp.concatenate([x1 * cos - x2 * sin, x2 * cos + x1 * sin], axis=-1)


def retention_rotate(x):
    S, d = x.shape[1], x.shape[-1]
    angle = 1.0 / (RET_ROT_BASE ** jnp.linspace(0.0, 1.0, d // 2, dtype=jnp.float32))
    ang = jnp.arange(S, dtype=jnp.float32)[:, None] * angle[None, :]
    cos = jnp.cos(ang)[None, :, None, :].astype(x.dtype)
    sin = jnp.sin(ang)[None, :, None, :].astype(x.dtype)
    xp = x.reshape(*x.shape[:-1], d // 2, 2)
    x0, x1 = xp[..., 0], xp[..., 1]
    return jnp.stack([x0 * cos - x1 * sin, x1 * cos + x0 * sin], axis=-1).reshape(x.shape)


def diff_attention(h, w_in, lq1, lk1, lq2, lk2, subln_g, w_out, lambda_init):
    B, S, _ = h.shape
    proj = h @ w_in
    q, k, v, gate = jnp.split(proj, [DA_QK_WIDTH, 2 * DA_QK_WIDTH, 2 * DA_QK_WIDTH + D_INNER], axis=-1)
    q = rotary_half(q.reshape(B, S, 2 * N_HEADS, DA_HEAD_DIM)) * (DA_HEAD_DIM ** -0.5)
    k = rotary_half(k.reshape(B, S, 2 * N_HEADS, DA_HEAD_DIM))
    v = v.reshape(B, S, N_HEADS, DA_V_DIM)
    f32 = jnp.float32
    lam = (jnp.exp(jnp.sum(lq1.astype(f32) * lk1.astype(f32)))
           - jnp.exp(jnp.sum(lq2.astype(f32) * lk2.astype(f32))) + lambda_init)
    n_blk = S // Q_BLOCK
    q_blocks = q.reshape(B, n_blk, Q_BLOCK, 2 * N_HEADS, DA_HEAD_DIM).transpose(1, 0, 2, 3, 4)

    def block(qi):
        s = jnp.einsum('bqhd,bkhd->bhqk', qi, k).astype(f32)
        p = jax.nn.softmax(s, axis=-1).reshape(B, N_HEADS, 2, Q_BLOCK, S)
        a = (p[:, :, 0] - lam * p[:, :, 1]).astype(v.dtype)
        return jnp.einsum('bhqk,bkhe->bqhe', a, v)

    o = lax.map(block, q_blocks)
    o = o.transpose(1, 0, 2, 3, 4).reshape(B, S, N_HEADS, DA_V_DIM)
    o = rms_norm(o, subln_g, SUBLN_EPS) * (1.0 - lambda_init)
    o = o.reshape(B, S, D_INNER) * jax.nn.silu(gate)
    return o @ w_out


def retention_direction(q, k, v, log_g, include_diag):
    B, S, H, dk = q.shape
    dv = v.shape[-1]
    C = RET_CHUNK
    n_chunk = S // C
    dt = q.dtype
    idx = jnp.arange(C, dtype=jnp.float32)
    diff = idx[:, None] - idx[None, :]
    mask = (diff >= 0) if include_diag else (diff > 0)
    intra = jnp.where(mask[None], jnp.exp(jnp.where(mask, diff, 0.0)[None] * log_g[:, None, None]), 0.0).astype(dt)
    q_decay = jnp.exp((idx + 1.0)[:, None] * log_g[None, :]).astype(dt)
    k_decay = jnp.exp((C - 1.0 - idx)[:, None] * log_g[None, :]).astype(dt)
    chunk_decay = jnp.exp(C * log_g).astype(dt)

    def to_chunks(t):
        return t.reshape(B, n_chunk, C, H, t.shape[-1]).transpose(1, 0, 2, 3, 4)

    def step(state, inp):
        qc, kc, vc = inp
        scores = jnp.einsum('bqhd,bkhd->bhqk', qc, kc) * intra[None]
        o = (jnp.einsum('bhqk,bkhe->bqhe', scores, vc)
             + jnp.einsum('bqhd,bhde->bqhe', qc * q_decay[None, :, :, None], state))
        state = (state * chunk_decay[None, :, None, None]
                 + jnp.einsum('bkhd,bkhe->bhde', kc * k_decay[None, :, :, None], vc))
        return state, o

    state0 = jnp.zeros((B, H, dk, dv), dt)
    _, o = lax.scan(step, state0, (to_chunks(q), to_chunks(k), to_chunks(v)))
    return o.transpose(1, 0, 2, 3, 4).reshape(B, S, H, dv)


def bidir_retention(h, w_in, decay_fwd, decay_bwd, subln_g, w_out):
    B, S, _ = h.shape
    proj = h @ w_in
    q, k, v, gate = jnp.split(proj, [RET_QK_WIDTH, 2 * RET_QK_WIDTH, 2 * RET_QK_WIDTH + D_INNER], axis=-1)
    q = retention_rotate(q.reshape(B, S, N_HEADS, RET_QK_DIM))
    k = retention_rotate(k.reshape(B, S, N_HEADS, RET_QK_DIM)) * (RET_QK_DIM ** -0.5)
    v = v.reshape(B, S, N_HEADS, RET_V_DIM)
    log_g_f = -jnp.exp(decay_fwd.astype(jnp.float32))
    log_g_b = -jnp.exp(decay_bwd.astype(jnp.float32))
    o_f = retention_direction(q, k, v, log_g_f, True)
    o_b = jnp.flip(retention_direction(jnp.flip(q, 1), jnp.flip(k, 1), jnp.flip(v, 1), log_g_b, False), 1)
    o = rms_norm(o_f + o_b, subln_g, SUBLN_EPS)
    o = o.reshape(B, S, D_INNER) * jax.nn.silu(gate)
    return o @ w_out


def trunk(x, norm_g, da_w_in, da_lambda_q1, da_lambda_k1, da_lambda_q2, da_lambda_k2,
          da_subln_g, da_w_out, ret_w_in, ret_decay_fwd, ret_decay_bwd, ret_subln_g,
          ret_w_out, final_norm_g):
    for i in range(DEPTH):
        h = rms_norm(x, norm_g[i], NORM_EPS)
        j = i // N_MIXERS
        if i % N_MIXERS == 0:
            lambda_init = 0.8 - 0.6 * math.exp(-0.3 * i)
            x = x + diff_attention(h, da_w_in[j], da_lambda_q1[j], da_lambda_k1[j], da_lambda_q2[j],
                                   da_lambda_k2[j], da_subln_g[j], da_w_out[j], lambda_init)
        else:
            x = x + bidir_retention(h, ret_w_in[j], ret_decay_fwd[j], ret_decay_bwd[j],
                                    ret_subln_g[j], ret_w_out[j])
    return rms_norm(x, final_norm_g, NORM_EPS)


def setup_inputs(seed: int = 0) -> dict:
    key = jax.random.key(seed)
    ks = jax.random.split(key, 16)
    f32 = jnp.float32
    base_decay = jnp.log(-jnp.log1p(-(2.0 ** (-5.0 - jnp.arange(N_HEADS, dtype=f32)))))
    return {
        "x_prompt": jax.random.normal(ks[0], (BATCH, SEQ, D_MODEL), f32),
        "x_sample": jax.random.normal(ks[1], (DEC_BATCH, DEC_SEQ, D_MODEL), f32),
        "norm_g": 1.0 + 0.02 * jax.random.normal(ks[2], (DEPTH, D_MODEL), f32),
        "da_w_in": jax.random.normal(ks[3], (N_ATTN_LAYERS, D_MODEL, DA_IN), f32) * D_MODEL ** -0.5,
        "da_lambda_q1": 0.1 * jax.random.normal(ks[4], (N_ATTN_LAYERS, DA_HEAD_DIM), f32),
        "da_lambda_k1": 0.1 * jax.random.normal(ks[5], (N_ATTN_LAYERS, DA_HEAD_DIM), f32),
        "da_lambda_q2": 0.1 * jax.random.normal(ks[6], (N_ATTN_LAYERS, DA_HEAD_DIM), f32),
        "da_lambda_k2": 0.1 * jax.random.normal(ks[7], (N_ATTN_LAYERS, DA_HEAD_DIM), f32),
        "da_subln_g": 1.0 + 0.02 * jax.random.normal(ks[8], (N_ATTN_LAYERS, DA_V_DIM), f32),
        "da_w_out": jax.random.normal(ks[9], (N_ATTN_LAYERS, D_INNER, D_MODEL), f32) * D_INNER ** -0.5,
        "ret_w_in": jax.random.normal(ks[10], (N_RET_LAYERS, D_MODEL, RET_IN), f32) * D_MODEL ** -0.5,
        "ret_decay_fwd": base_decay[None] + 0.1 * jax.random.normal(ks[11], (N_RET_LAYERS, N_HEADS), f32),
        "ret_decay_bwd": base_decay[None] + 0.1 * jax.random.normal(ks[12], (N_RET_LAYERS, N_HEADS), f32),
        "ret_subln_g": 1.0 + 0.02 * jax.random.normal(ks[13], (N_RET_LAYERS, RET_V_DIM), f32),
        "ret_w_out": jax.random.normal(ks[14], (N_RET_LAYERS, D_INNER, D_MODEL), f32) * D_INNER ** -0.5,
        "final_norm_g": 1.0 + 0.02 * jax.random.normal(ks[15], (D_MODEL,), f32),
    }


def reference(x_prompt, x_sample, norm_g, da_w_in, da_lambda_q1, da_lambda_k1, da_lambda_q2,
              da_lambda_k2, da_subln_g, da_w_out, ret_w_in, ret_decay_fwd, ret_decay_bwd,
              ret_subln_g, ret_w_out, final_norm_g):
    y_prompt = trunk(x_prompt, norm_g, da_w_in, da_lambda_q1, da_lambda_k1, da_lambda_q2, da_lambda_k2,
                     da_subln_g, da_w_out, ret_w_in, ret_decay_fwd, ret_decay_bwd, ret_subln_g,
                     ret_w_out, final_norm_g)
    y_sample = trunk(x_sample, norm_g, da_w_in, da_lambda_q1, da_lambda_k1, da_lambda_q2, da_lambda_k2,
                     da_subln_g, da_w_out, ret_w_in, ret_decay_fwd, ret_decay_bwd, ret_subln_g,
                     ret_w_out, final_norm_g)
    return (y_prompt, y_sample)
```

```python
import numpy as np
from contextlib import ExitStack
import concourse.bass as bass
import concourse.mybir as mybir
from concourse.bass_utils import run_bass_kernel_spmd

F32 = mybir.dt.float32
BF16 = mybir.dt.bfloat16
AF = mybir.ActivationFunctionType
ALU = mybir.AluOpType

S = 4096
D = 1024
NT = 32
NH = 8
ENGS = ("pe", "act", "dve", "pool", "sp")
QSCALE = 128.0 ** -0.5
LAMBDA_INIT0 = 0.8 - 0.6 * 1.0


class Op:
    __slots__ = ("eng", "fn", "deps", "signal", "semkey", "val", "is_dma", "idx")


class Planner:
    def __init__(self):
        self.ops = {e: [] for e in ENGS}
        self.epoch = 0
        self.last_w = {}
        self.readers = {}
        self.dma_last = {}
        self.dma_eng = {}
        self.bar_deps = []
        self.bar_need = {e: False for e in ENGS}

    def add(self, eng, fn, reads=(), writes=(), dma=None, extra=()):
        op = Op()
        op.eng = eng
        op.fn = fn
        op.is_dma = dma is not None
        op.signal = op.is_dma
        op.val = None
        op.semkey = ("dma", dma) if op.is_dma else (eng, self.epoch)
        if op.is_dma:
            assert self.dma_eng.setdefault(dma, eng) == eng
        deps = {}

        def dep(d, raw):
            if d is None:
                return
            if (not d.is_dma) and (not op.is_dma) and d.eng == eng:
                if eng == "pe" or (not raw and eng != "pool"):
                    return
            deps[id(d)] = d

        for r in reads:
            dep(self.last_w.get(r), True)
        for w in writes:
            dep(self.last_w.get(w), False)
            for rd in self.readers.get(w, ()):
                dep(rd, False)
        for d in extra:
            dep(d, True)
        if self.bar_need[eng]:
            for d in self.bar_deps:
                dep(d, True)
            self.bar_need[eng] = False
        best = {}
        for d in deps.values():
            b = best.get(d.semkey)
            if b is None or d.idx > b.idx:
                best[d.semkey] = d
        op.deps = list(best.values())
        for d in op.deps:
            d.signal = True
        op.idx = len(self.ops[eng])
        self.ops[eng].append(op)
        for r in reads:
            self.readers.setdefault(r, []).append(op)
        for w in writes:
            self.last_w[w] = op
            self.readers[w] = []
        if op.is_dma:
            self.dma_last[dma] = op
        return op

    def barrier(self):
        deps = []
        for e in ENGS:
            for op in reversed(self.ops[e]):
                if not op.is_dma:
                    deps.append(op)
                    break
        deps.extend(self.dma_last.values())
        self.bar_deps = deps
        for e in ENGS:
            self.bar_need[e] = True

    def new_epoch(self):
        self.epoch += 1

    def semkeys(self):
        keys = []
        seen = set()
        for e in ENGS:
            for op in self.ops[e]:
                if op.signal and op.semkey not in seen:
                    seen.add(op.semkey)
                    keys.append(op.semkey)
        return keys

    def assign(self):
        cnt = {}
        for e in ENGS:
            for op in self.ops[e]:
                if op.signal:
                    inc = 16 if op.is_dma else 1
                    cnt[op.semkey] = cnt.get(op.semkey, 0) + inc
                    op.val = cnt[op.semkey]
        return cnt

    def emit_engine(self, e, eng, sems):
        waited = {}
        for op in self.ops[e]:
            need = {}
            for d in op.deps:
                if d.val > need.get(d.semkey, 0):
                    need[d.semkey] = d.val
            for k, v in need.items():
                if waited.get(k, 0) < v:
                    eng.wait_ge(sems[k], v)
                    waited[k] = v
            inst = op.fn(eng)
            if op.signal and inst is not None:
                inst.then_inc(sems[op.semkey], 16 if op.is_dma else 1)


def build(NSEQ=3, do_layers=(0, 1), attn_qb=16, dbg=False):
    nc = bass.Bass("TRN2", target_bir_lowering=False)
    dt_in = lambda name, shape: nc.dram_tensor(name, shape, F32, kind="ExternalInput").ap()
    x_d = dt_in("x", [NSEQ, S, D])
    win0_d = dt_in("w_in0", [NH, 128, 8192])
    wout0_d = dt_in("w_out0", [128, 16384])
    win1_d = dt_in("w_in1", [NH, 128, 6144])
    wout1_d = dt_in("w_out1", [128, 16384])
    g0_d = dt_in("g0", [128, D])
    g1_d = dt_in("g1", [128, D])
    gf_d = dt_in("gf", [128, D])
    sub_d = dt_in("sub", [128, 512])
    lam_d = dt_in("lam4", [128, 512])
    dec_d = dt_in("dec", [128, 16])
    tab0_d = dt_in("tab0", [128, 8192])
    tab1_d = dt_in("tab1", [128, 8192])
    cm_d = dt_in("cmask", [128, 6 * 128])
    cj_d = dt_in("cj", [128, 2])
    y_d = nc.dram_tensor("y", [NSEQ, S, D], F32, kind="ExternalOutput").ap()
    og_d = nc.dram_tensor("og_scr", [NSEQ * 2, S, 2048], BF16).ap()
    x1_d = (nc.dram_tensor("x1_scr", [NSEQ, S, D], F32, kind="ExternalOutput").ap() if dbg
            else nc.dram_tensor("x1_scr", [NSEQ, S, D], F32).ap())

    P = Planner()
    with ExitStack() as es:
        E = es.enter_context
        hT = E(nc.sbuf_tensor("hT", [128, 8, S], BF16))
        ARENA = E(nc.sbuf_tensor("arena", [128, 49280], BF16))
        A = ARENA
        QT = A[:, 0:8192].rearrange("p (c t) -> p c t", c=2)
        KT = A[:, 8192:16384].rearrange("p (c t) -> p c t", c=2)
        Vb = A[:, 16384:16384 + 8256].rearrange("p (t e) -> p t e", e=258)
        GS = A[:, 24640:24640 + 8192].rearrange("p (t e) -> p t e", e=256)
        Wh = A[:, 32832:32832 + 8192].rearrange("p (c n) -> p c n", n=1024)
        Wh_flat = A[:, 32832:32832 + 8192]
        TAB = A[:, 41024:41024 + 8192]
        COS = A[:, 41024:41024 + 4096]
        SIN = A[:, 41024 + 4096:41024 + 8192]
        SF = None
        Wout = A[:, 32832:49216].rearrange("p (j n) -> p j n", n=1024)
        Wout_flat = A[:, 32832:49216]
        ogin = [A[:, 16384 + i * 2048:16384 + (i + 1) * 2048] for i in range(2)]
        ogT = [A[:, 20480 + i * 2048:20480 + (i + 1) * 2048].rearrange("p (j t) -> p j t", t=128) for i in range(2)]
        xs = [A[:, 24576 + i * 1024:24576 + (i + 1) * 1024] for i in range(2)]
        junk = A[:, 26624:26624 + 1024]
        xin = [A[:, 27648 + i * 2048:27648 + (i + 1) * 2048].bitcast(F32) for i in range(2)]
        x1t = [A[:, i * 2048:(i + 1) * 2048].bitcast(F32) for i in range(2)]
        grep_ = A[:, 4096:6144].bitcast(F32)
        yt = [A[:, 6144 + i * 2048:6144 + (i + 1) * 2048].bitcast(F32) for i in range(2)]

        SCR = E(nc.sbuf_tensor("scr", [128, 8192], BF16))
        PT = [SCR[:, i * 1024:(i + 1) * 1024] for i in range(3)]
        SF = SCR[:, 0:8192].rearrange("p (t e) -> p t e", e=256)
        t1b = [E(nc.sbuf_tensor(f"t1b{i}", [128, 512], F32)) for i in range(2)]
        t2b = [E(nc.sbuf_tensor(f"t2b{i}", [128, 512], F32)) for i in range(2)]
        sgb = [E(nc.sbuf_tensor(f"sgb{i}", [128, 256], F32)) for i in range(2)]
        oe = [SCR[:, 3072 + i * 1032:3072 + (i + 1) * 1032].bitcast(F32).rearrange("p (a b) -> p a b", a=2) for i in range(2)]
        ttb = [SCR[:, 5136 + i * 512:5136 + (i + 1) * 512].bitcast(F32) for i in range(2)]
        ocb = [SCR[:, 6160 + i * 512:6160 + (i + 1) * 512].bitcast(F32) for i in range(2)]
        ogst = [E(nc.sbuf_tensor(f"ogst{i}", [128, 256], BF16)) for i in range(4)]
        junk2 = E(nc.sbuf_tensor("junk2", [128, 256], BF16))
        junk3 = SCR[:, 7440:7696]
        small = E(nc.sbuf_tensor("small", [128, 64], F32))
        ident = E(nc.sbuf_tensor("ident", [128, 128], BF16))
        subg = E(nc.sbuf_tensor("subg", [128, 512], F32))
        lam4 = E(nc.sbuf_tensor("lam4s", [128, 512], F32))
        lamj = E(nc.sbuf_tensor("lamj", [128, 128], F32))
        dec = E(nc.sbuf_tensor("decs", [128, 16], F32))
        lg = E(nc.sbuf_tensor("lg", [128, 16], F32))
        cm = E(nc.sbuf_tensor("cm", [128, 6, 128], F32))
        cj = E(nc.sbuf_tensor("cjs", [128, 2], F32))
        consts = E(nc.sbuf_tensor("consts", [128, 16], F32))
        DT1 = E(nc.sbuf_tensor("DT1", [128, 128], F32))
        DT2 = E(nc.sbuf_tensor("DT2", [128, 128], F32))
        DTm = E(nc.sbuf_tensor("DTm", [128, 128], F32))
        DFq = E(nc.sbuf_tensor("DFq", [128, 128], BF16))
        DBq = E(nc.sbuf_tensor("DBq", [128, 128], BF16))
        hk = E(nc.sbuf_tensor("hk", [128, 8], F32))
        Kfb = [E(nc.sbuf_tensor(f"Kfb{i}", [128, 128], BF16)) for i in range(2)]
        PTr = [E(nc.sbuf_tensor(f"PTr{i}", [128, 128], BF16)) for i in range(2)]
        Sst = [E(nc.sbuf_tensor(f"Sst{i}", [128, 256], F32)) for i in range(2)]
        SBb = [E(nc.sbuf_tensor(f"SBb{i}", [128, 256], BF16)) for i in range(2)]
        PS = E(nc.psum_tensor("ps_all", [128, 4096], F32))
        BANK = [PS[:, i * 512:(i + 1) * 512] for i in range(8)]

        def sm(i):
            return small[:, i:i + 1]
        EPS6, EPS5, NEGLAM = consts[:, 0:1], consts[:, 1:2], consts[:, 2:3]

        def bk(i):
            return ("bank", i)

        P.add("pool", lambda e: e.memset(ident[:], 0.0), writes=["ident"])
        P.add("pool", lambda e: e.affine_select(out=ident[:], in_=ident[:], pattern=[[-1, 128]],
                                                compare_op=ALU.not_equal, fill=1.0, base=0,
                                                channel_multiplier=1), reads=["ident"], writes=["ident"])
        P.add("pool", lambda e: e.memset(consts[:, 0:1], 1e-6), writes=["c0"])
        P.add("pool", lambda e: e.memset(consts[:, 1:2], 1e-5), writes=["c1"])
        P.add("pool", lambda e: e.memset(small[:], 0.0), writes=["small"])
        for (dst, src, nm) in ((subg, sub_d, "sub"), (lam4, lam_d, "lam"), (dec, dec_d, "dec"),
                               (cj, cj_d, "cj")):
            P.add("sp", lambda e, dst=dst, src=src: e.dma_start(out=dst[:], in_=src[:, :]),
                  writes=[nm], dma="init_" + nm)
        P.add("sp", lambda e: e.dma_start(out=cm[:].rearrange("p a b -> p (a b)"), in_=cm_d[:, :]),
              writes=["cm"], dma="init_cm")
        P.add("dve", lambda e: e.tensor_scalar(subg[:, 0:256], subg[:, 0:256], 1.0 - LAMBDA_INIT0, None, ALU.mult),
              reads=["sub"], writes=["sub"])
        P.add("dve", lambda e: e.scalar_tensor_tensor(out=lamj[:], in0=lam4[:, 0:128], scalar=1.0, in1=lam4[:, 128:256],
                                                      op0=ALU.mult, op1=ALU.mult, accum_out=small[:, 60:61]),
              reads=["lam", "small"], writes=["lamj", "d1"])
        P.add("dve", lambda e: e.scalar_tensor_tensor(out=lamj[:], in0=lam4[:, 256:384], scalar=1.0, in1=lam4[:, 384:512],
                                                      op0=ALU.mult, op1=ALU.mult, accum_out=small[:, 61:62]),
              reads=["lam", "lamj", "small"], writes=["lamj", "d2"])
        P.add("act", lambda e: e.activation(out=small[:, 62:64], in_=small[:, 60:62], func=AF.Exp),
              reads=["d1", "d2"], writes=["e12"])
        P.add("dve", lambda e: e.tensor_tensor(consts[:, 2:3], small[:, 63:64], small[:, 62:63], ALU.subtract),
              reads=["e12"], writes=["neglam0"])
        P.add("dve", lambda e: e.tensor_scalar(consts[:, 2:3], consts[:, 2:3], -LAMBDA_INIT0, None, ALU.add),
              reads=["neglam0"], writes=["neglam"])
        P.add("act", lambda e: e.activation(out=lg[:], in_=dec[:], func=AF.Exp), reads=["dec"], writes=["lg0"])
        P.add("dve", lambda e: e.tensor_scalar(lg[:], lg[:], -1.0, None, ALU.mult), reads=["lg0"], writes=["lg"])

        def norm_tile(src_ap, src_key, t, slot, hbank):
            ssq, lnv, rstd = sm(slot * 4 + 0), sm(slot * 4 + 1), sm(slot * 4 + 2)
            P.add("act", lambda e: e.activation(out=junk, in_=src_ap, func=AF.Square, accum_out=ssq),
                  reads=[src_key], writes=["junk", ("ssq", slot)])
            P.add("act", lambda e: e.activation(out=lnv, in_=ssq, func=AF.Ln, scale=1.0 / D, bias=EPS6),
                  reads=[("ssq", slot), "c0"], writes=[("lnv", slot)])
            P.add("act", lambda e: e.activation(out=rstd, in_=lnv, func=AF.Exp, scale=-0.5),
                  reads=[("lnv", slot)], writes=[("rstd", slot)])
            P.add("dve", lambda e: e.scalar_tensor_tensor(out=xs[slot], in0=src_ap, scalar=rstd, in1=grep_,
                                                          op0=ALU.mult, op1=ALU.mult),
                  reads=[src_key, ("rstd", slot), "grep"], writes=[("xs", slot)])
            b0 = 4 + 2 * slot
            bv = PS[:, b0 * 512:(b0 + 2) * 512].rearrange("p (c t) -> p c t", c=8)
            for c in range(8):
                P.add("pe", lambda e, c=c: e.matmul(bv[:, c, :], lhsT=xs[slot][:, c * 128:(c + 1) * 128], rhs=ident[:], start=True, stop=True),
                      reads=[("xs", slot), "ident"], writes=[bk(b0 + c // 4)])
            P.add("act", lambda e: e.activation(out=hT[:, :, t * 128:(t + 1) * 128], in_=bv, func=AF.Copy),
                  reads=[bk(b0), bk(b0 + 1)], writes=[("hT", t)])

        def load_gain(src_d):
            P.add("sp", lambda e: e.dma_start(out=grep_, in_=src_d[:, :]), writes=["grep"], dma="grep")

        def phase0(s):
            xin4 = [A[:, i * 2048:(i + 1) * 2048].bitcast(F32) for i in range(4)]
            xs4 = [A[:, 8192 + i * 1024:8192 + (i + 1) * 1024] for i in range(4)]
            junk = A[:, 12288:13312]
            grep_ = A[:, 13312:15360].bitcast(F32)
            P.add("sp", lambda e: e.dma_start(out=grep_, in_=g0_d[:, :]), writes=["grepp0"], dma="grepp0")

            def L(t):
                k = t % 4
                P.add("sp", lambda e: e.dma_start(out=xin4[k], in_=x_d[s, t * 128:(t + 1) * 128, :]),
                      writes=[("xin4", k)], dma=f"xin4_{k}")

            def N1(t):
                k = t % 4
                ssq, lnv, rstd = sm(k * 4 + 0), sm(k * 4 + 1), sm(k * 4 + 2)
                P.add("act", lambda e: e.activation(out=junk, in_=xin4[k], func=AF.Square, accum_out=ssq),
                      reads=[("xin4", k)], writes=["junk", ("ssq4", k)])
                P.add("act", lambda e: e.activation(out=lnv, in_=ssq, func=AF.Ln, scale=1.0 / D, bias=EPS6),
                      reads=[("ssq4", k), "c0"], writes=[("lnv4", k)])
                P.add("act", lambda e: e.activation(out=rstd, in_=lnv, func=AF.Exp, scale=-0.5),
                      reads=[("lnv4", k)], writes=[("rstd4", k)])
                P.add("dve", lambda e: e.scalar_tensor_tensor(out=xs4[k], in0=xin4[k], scalar=rstd, in1=grep_,
                                                              op0=ALU.mult, op1=ALU.mult),
                      reads=[("xin4", k), ("rstd4", k), "grepp0"], writes=[("xs4", k)])

            def N2(t):
                k = t % 4
                b0 = 2 * k
                bv = PS[:, b0 * 512:(b0 + 2) * 512].rearrange("p (c t) -> p c t", c=8)
                for c in range(8):
                    P.add("pe", lambda e, c=c: e.matmul(bv[:, c, :], lhsT=xs4[k][:, c * 128:(c + 1) * 128], rhs=ident[:], start=True, stop=True),
                          reads=[("xs4", k), "ident"], writes=[bk(b0 + c // 4)])
                P.add("act", lambda e: e.activation(out=hT[:, :, t * 128:(t + 1) * 128], in_=bv, func=AF.Copy),
                      reads=[bk(b0), bk(b0 + 1)], writes=[("hT", t)])

            for t in range(3):
                L(t)
            for t in range(NT):
                if t + 3 < NT:
                    L(t + 3)
                N1(t)
                if t >= 1:
                    N2(t - 1)
            N2(NT - 1)

        def rope_evac(bank, dst, tb, slot):
            ps = BANK[bank]
            cs = slice(tb * 512, (tb + 1) * 512)
            P.add("dve", lambda e: e.tensor_tensor(t1b[slot][:], ps[:], COS[:, cs], ALU.mult),
                  reads=[bk(bank), ("tab", 0), ("tab", 1)], writes=[("t1", slot)])
            P.add("dve", lambda e: e.tensor_tensor(t2b[slot][0:64, :], ps[64:128, :], SIN[0:64, cs], ALU.mult),
                  reads=[bk(bank), ("tab", 0), ("tab", 1)], writes=[("t2a", slot)])
            P.add("dve", lambda e: e.tensor_tensor(t2b[slot][64:128, :], ps[0:64, :], SIN[64:128, cs], ALU.mult),
                  reads=[bk(bank), ("tab", 0), ("tab", 1)], writes=[("t2b", slot)])
            P.add("pool", lambda e: e.tensor_tensor(dst, t1b[slot][:], t2b[slot][:], ALU.add),
                  reads=[("t1", slot), ("t2a", slot), ("t2b", slot)], writes=[])
            return

        def projection(layer, h, s, skip_w=False):
            if layer == 0:
                if not skip_w:
                    P.add("pool", lambda e: e.dma_start(out=Wh_flat, in_=win0_d[h, :, :]),
                          writes=["Wh"], dma="Wh")
                nrow, vg0 = 4, 512
            else:
                P.add("pool", lambda e: e.dma_start(out=Wh[:, :, 0:768],
                                                    in_=win1_d[h, :, :].rearrange("p (c n) -> p c n", n=768)),
                      writes=["Wh"], dma="Wh")
                nrow, vg0 = 2, 256
            gsub = subg[:, 0:256] if layer == 0 else subg[:, 256:512]
            cnt = {"q": 0, "v": 0}

            def qk_block(r, tb):
                bank = cnt["q"] % 2
                slot = cnt["q"] % 2
                cnt["q"] += 1
                for c in range(8):
                    P.add("pe", lambda e, c=c: e.matmul(
                        BANK[bank][:], lhsT=Wh[:, c, r * 128:(r + 1) * 128], rhs=hT[:, c, tb * 512:(tb + 1) * 512],
                        start=(c == 0), stop=(c == 7)),
                        reads=["Wh"] + [("hT", tb * 4 + i) for i in range(4)], writes=[bk(bank)])
                if layer == 0:
                    dst_t, sub = (QT, r) if r < 2 else (KT, r - 2)
                else:
                    dst_t, sub = (QT, 0) if r == 0 else (KT, 0)
                dkey = ("QT" if dst_t is QT else "KT", sub, tb)
                dst = dst_t[:, sub, tb * 512:(tb + 1) * 512]
                ps = BANK[bank]
                cs = slice(tb * 512, (tb + 1) * 512)
                tabk = [("tab", 0), ("tab", 1)]
                P.add("dve", lambda e: e.tensor_tensor(t1b[slot][:], ps[:], COS[:, cs], ALU.mult),
                      reads=[bk(bank)] + tabk, writes=[("t1", slot)])
                P.add("dve", lambda e: e.tensor_tensor(t2b[slot][0:64, :], ps[64:128, :], SIN[0:64, cs], ALU.mult),
                      reads=[bk(bank)] + tabk, writes=[("t2a", slot)])
                P.add("dve", lambda e: e.tensor_tensor(t2b[slot][64:128, :], ps[0:64, :], SIN[64:128, cs], ALU.mult),
                      reads=[bk(bank)] + tabk, writes=[("t2b", slot)])
                P.add("pool", lambda e: e.tensor_tensor(dst, t1b[slot][:], t2b[slot][:], ALU.add),
                      reads=[("t1", slot), ("t2a", slot), ("t2b", slot)], writes=[dkey])

            def vg_tile(t):
                bank = 2 + cnt["v"] % 2
                slot = cnt["v"] % 2
                cnt["v"] += 1
                for c in range(8):
                    P.add("pe", lambda e, c=c: e.matmul(
                        BANK[bank][:], lhsT=hT[:, c, t * 128:(t + 1) * 128], rhs=Wh[:, c, vg0:vg0 + 512],
                        start=(c == 0), stop=(c == 7)),
                        reads=["Wh", ("hT", t)], writes=[bk(bank)])
                P.add("act", lambda e: e.activation(out=Vb[:, t, 0:256], in_=BANK[bank][:, 0:256], func=AF.Copy),
                      reads=[bk(bank)], writes=[("V", t)])
                P.add("act", lambda e: e.activation(out=sgb[slot][:], in_=BANK[bank][:, 256:512], func=AF.Silu),
                      reads=[bk(bank)], writes=[("sg", slot)])
                P.add("pool", lambda e: e.tensor_tensor(GS[:, t, :], sgb[slot][:], gsub, ALU.mult),
                      reads=[("sg", slot), "sub"], writes=[("GS", t)])

            qk_list = [(r, tb) for r in range(nrow) for tb in range(8)]
            per = NT // len(qk_list)
            t = 0
            for (r, tb) in qk_list:
                qk_block(r, tb)
                for _ in range(per):
                    vg_tile(t)
                    t += 1

        og_cnt = [0]

        def attention(h, s):
            NQB = attn_qb
            nsteps = NQB * NT

            def QK(i):
                qb, kt = divmod(i, NT)
                b = i % 4
                for sh in range(2):
                    P.add("pe", lambda e, sh=sh, qb=qb, kt=kt, b=b: e.matmul(
                        BANK[b][:, sh * 256:(sh + 1) * 256], lhsT=KT[:, sh, kt * 128:(kt + 1) * 128],
                        rhs=QT[:, sh, qb * 256:(qb + 1) * 256], start=True, stop=True),
                        reads=[("KT", sh, kt // 4), ("QT", sh, qb // 2)], writes=[bk(b)])

            def EXPG(j):
                g = j % 2
                p = j % 3
                P.add("act", lambda e, g=g, p=p: e.activation(out=PT[p][:], in_=PS[:, g * 1024:(g + 1) * 1024], func=AF.Exp, scale=QSCALE),
                      reads=[bk(2 * g), bk(2 * g + 1)], writes=[("PT", p)])

            def PV(i):
                qb, kt = divmod(i, NT)
                p = (i // 2) % 3
                off = (i % 2) * 512
                for sh in range(2):
                    for qt in range(2):
                        a = 4 + sh * 2 + qt
                        P.add("pe", lambda e, sh=sh, qt=qt, a=a, kt=kt, p=p, off=off: e.matmul(
                            BANK[a][:, 0:257], lhsT=PT[p][:, off + sh * 256 + qt * 128: off + sh * 256 + (qt + 1) * 128],
                            rhs=Vb[:, kt, 0:257], start=(kt == 0), stop=(kt == NT - 1)),
                            reads=[("PT", p), ("V", kt), "Vones"], writes=[bk(a)])

            def EPI_EVAC(qb):
                for sh in range(2):
                    for qt in range(2):
                        a = 4 + sh * 2 + qt
                        sl = (qb * 2 + qt) % 2
                        if sh == 0:
                            P.add("dve", lambda e, sl=sl, a=a, sh=sh: e.tensor_copy(oe[sl][:, sh, 0:257], BANK[a][:, 0:257]),
                                  reads=[bk(a)], writes=[("oe%d" % sh, sl)])
                        else:
                            P.add("act", lambda e, sl=sl, a=a, sh=sh: e.activation(out=oe[sl][:, sh, 0:257], in_=BANK[a][:, 0:257], func=AF.Copy),
                                  reads=[bk(a)], writes=[("oe%d" % sh, sl)])

            def EPI_A(qb, qt):
                tile = qb * 2 + qt
                sl = tile % 2
                rs = small[:, 16 + sl * 8: 16 + sl * 8 + 2]
                nl1 = small[:, 16 + sl * 8 + 2: 16 + sl * 8 + 3]
                P.add("dve", lambda e, sl=sl, rs=rs: e.reciprocal(rs, oe[sl][:, :, 256]),
                      reads=[("oe0", sl), ("oe1", sl)], writes=[("rs", sl)])
                P.add("dve", lambda e, rs=rs, nl1=nl1: e.tensor_tensor(nl1, rs[:, 1:2], NEGLAM, ALU.mult),
                      reads=[("rs", sl), "neglam"], writes=[("nl1", sl)])
                P.add("dve", lambda e, sl=sl, nl1=nl1: e.tensor_scalar(ttb[sl][:], oe[sl][:, 1, 0:256], nl1, None, ALU.mult),
                      reads=[("oe1", sl), ("nl1", sl)], writes=[("tt", sl)])
                P.add("dve", lambda e, sl=sl, rs=rs: e.scalar_tensor_tensor(
                    out=ocb[sl][:], in0=oe[sl][:, 0, 0:256], scalar=rs[:, 0:1], in1=ttb[sl][:],
                    op0=ALU.mult, op1=ALU.add),
                    reads=[("oe0", sl), ("rs", sl), ("tt", sl)], writes=[("oc", sl)])
                ssq = small[:, 40 + qt: 41 + qt]
                P.add("dve", lambda e, sl=sl, ssq=ssq: e.scalar_tensor_tensor(
                    out=junk3[:], in0=ocb[sl][:], scalar=1.0, in1=ocb[sl][:], op0=ALU.mult, op1=ALU.mult, accum_out=ssq),
                    reads=[("oc", sl)], writes=["junk3", ("essq2", qt)])

            def EPI_B(qb):
                P.add("act", lambda e: e.activation(out=small[:, 42:44], in_=small[:, 40:42], func=AF.Ln, scale=1.0 / 256, bias=EPS5),
                      reads=[("essq2", 0), ("essq2", 1), "c1"], writes=["elnv2"])
                P.add("act", lambda e: e.activation(out=small[:, 44:46], in_=small[:, 42:44], func=AF.Exp, scale=-0.5),
                      reads=["elnv2"], writes=["erstd2"])
                for qt in range(2):
                    tile = qb * 2 + qt
                    sl = tile % 2
                    o4 = og_cnt[0] % 4
                    og_cnt[0] += 1
                    rstd = small[:, 44 + qt: 45 + qt]
                    P.add("dve", lambda e, sl=sl, rstd=rstd, tile=tile, o4=o4: e.scalar_tensor_tensor(
                        out=ogst[o4][:], in0=ocb[sl][:], scalar=rstd, in1=GS[:, tile, :], op0=ALU.mult, op1=ALU.mult),
                        reads=[("oc", sl), "erstd2", ("GS", tile)], writes=[("ogst", o4)])
                    P.add("sp", lambda e, tile=tile, o4=o4: e.dma_start(
                        out=og_d[s * 2 + 0, tile * 128:(tile + 1) * 128, h * 256:(h + 1) * 256], in_=ogst[o4][:]),
                        reads=[("ogst", o4)], writes=[("ogd", s, 0, tile, h)], dma=f"ogst{o4}")

            pending = []

            npairs = nsteps // 2
            for i in range(4):
                QK(i)
            EXPG(0)
            for j in range(npairs):
                if j + 1 < npairs:
                    EXPG(j + 1)
                for i in (2 * j + 4, 2 * j + 5):
                    if i < nsteps:
                        QK(i)
                PV(2 * j)
                PV(2 * j + 1)
                while pending and pending[0][0] <= j:
                    pending.pop(0)[1]()
                if (2 * j + 1) % NT == NT - 1:
                    qb = (2 * j + 1) // NT
                    EPI_EVAC(qb)
                    EPI_A(qb, 0)
                    EPI_A(qb, 1)
                    pending.append((j + 6, lambda qb=qb: EPI_B(qb)))
            while pending:
                pending.pop(0)[1]()

        def prefetch_wout(layer):
            wsrc = wout0_d if layer == 0 else wout1_d
            alias = {0: ["Wh"], 1: ["Wh"], 2: [("tab", 0)], 3: [("tab", 1)]}
            for q in range(4):
                P.add("pool", lambda e, q=q: e.dma_start(out=Wout_flat[:, q * 4096:(q + 1) * 4096],
                                                         in_=wsrc[:, q * 4096:(q + 1) * 4096]),
                      writes=[("Wout", q)] + alias[q], dma=f"Wout{q}")

        def outproj(layer, s):
            P.barrier()
            load_gain(g1_d if layer == 0 else gf_d)
            tpv = PS[:, 0:2048].rearrange("p (c t) -> p c t", c=16)
            xsrc = x_d if layer == 0 else x1_d

            def stA(t):
                slot = t % 2
                rows = slice(t * 128, (t + 1) * 128)
                P.add("sp", lambda e: e.dma_start(out=ogin[slot], in_=og_d[s * 2 + layer, rows, :]),
                      reads=[("ogd", s, layer, t, hh) for hh in range(NH)], writes=[("ogin", slot)], dma=f"ogin{slot}")
                P.add("sp", lambda e: e.dma_start(out=xin[slot], in_=xsrc[s, rows, :]),
                      reads=[("x1d", s, t)], writes=[("xin", slot)], dma=f"xin{slot}")
                for j in range(16):
                    P.add("pe", lambda e, j=j: e.matmul(tpv[:, j, :], lhsT=ogin[slot][:, j * 128:(j + 1) * 128], rhs=ident[:], start=True, stop=True),
                          reads=[("ogin", slot), "ident"], writes=[bk(j // 4)])
                P.add("act", lambda e: e.activation(out=ogT[slot][:, 0:8, :], in_=tpv[:, 0:8, :], func=AF.Copy),
                      reads=[bk(0), bk(1)], writes=[("ogTa", slot)])
                P.add("dve", lambda e: e.tensor_copy(ogT[slot][:, 8:16, :], tpv[:, 8:16, :]),
                      reads=[bk(2), bk(3)], writes=[("ogTb", slot)])

            def stB(t):
                slot = t % 2
                rows = slice(t * 128, (t + 1) * 128)
                for half in range(2):
                    for j in range(16):
                        P.add("pe", lambda e, j=j, half=half: e.matmul(
                            BANK[4 + half][:], lhsT=ogT[slot][:, j, :], rhs=Wout[:, j, half * 512:(half + 1) * 512],
                            start=(j == 0), stop=(j == 15)),
                            reads=[("ogTa", slot), ("ogTb", slot), ("Wout", j // 4)], writes=[bk(4 + half)])
                for half in range(2):
                    hs = slice(half * 512, (half + 1) * 512)
                    P.add("dve", lambda e, half=half, hs=hs: e.tensor_tensor(x1t[slot][:, hs], BANK[4 + half][:], xin[slot][:, hs], ALU.add),
                          reads=[bk(4 + half), ("xin", slot)], writes=[("x1t", slot, half)])
                src_keys = [("x1t", slot, 0), ("x1t", slot, 1)]
                ssq, lnv, rstd = sm(slot * 4 + 0), sm(slot * 4 + 1), sm(slot * 4 + 2)
                if layer == 0:
                    P.add("pool", lambda e: e.dma_start(out=x1_d[s, rows, :], in_=x1t[slot]),
                          reads=src_keys, writes=[("x1d", s, t)], dma=f"x1st{slot}")
                P.add("act", lambda e: e.activation(out=junk, in_=x1t[slot], func=AF.Square, accum_out=ssq),
                      reads=src_keys, writes=["junk", ("ssq", slot)])
                P.add("act", lambda e: e.activation(out=lnv, in_=ssq, func=AF.Ln, scale=1.0 / D, bias=EPS6),
                      reads=[("ssq", slot), "c0"], writes=[("lnv", slot)])
                P.add("act", lambda e: e.activation(out=rstd, in_=lnv, func=AF.Exp, scale=-0.5),
                      reads=[("lnv", slot)], writes=[("rstd", slot)])
                if layer == 0:
                    P.add("dve", lambda e: e.scalar_tensor_tensor(
                        out=xs[slot], in0=x1t[slot], scalar=rstd, in1=grep_, op0=ALU.mult, op1=ALU.mult),
                        reads=src_keys + [("rstd", slot), "grep"], writes=[("xs", slot)])
                else:
                    P.add("dve", lambda e: e.scalar_tensor_tensor(
                        out=yt[slot], in0=x1t[slot], scalar=rstd, in1=grep_, op0=ALU.mult, op1=ALU.mult),
                        reads=src_keys + [("rstd", slot), "grep"], writes=[("yt", slot)])
                    P.add("pool", lambda e: e.dma_start(out=y_d[s, rows, :], in_=yt[slot]),
                          reads=[("yt", slot)], writes=[("yd", s, t)], dma=f"yst{slot}")

            def stC(t):
                slot = t % 2
                bv = PS[:, 3072:4096].rearrange("p (c t) -> p c t", c=8)
                for c in range(8):
                    P.add("pe", lambda e, c=c: e.matmul(bv[:, c, :], lhsT=xs[slot][:, c * 128:(c + 1) * 128], rhs=ident[:], start=True, stop=True),
                          reads=[("xs", slot), "ident"], writes=[bk(6 + c // 4)])
                P.add("act", lambda e: e.activation(out=hT[:, :, t * 128:(t + 1) * 128], in_=bv, func=AF.Copy),
                      reads=[bk(6), bk(7)], writes=[("hT", t)])

            stA(0)
            for t in range(NT):
                if t + 1 < NT:
                    stA(t + 1)
                stB(t)
                if layer == 0 and t >= 1:
                    stC(t - 1)
            if layer == 0:
                stC(NT - 1)
            P.barrier()

        def ret_consts(h):
            lgf, lgb = lg[:, h:h + 1], lg[:, 8 + h:9 + h]
            kdf, kdb, gCf, gCb = hk[:, 0:1], hk[:, 1:2], hk[:, 2:3], hk[:, 3:4]
            Mf, Mb, mkf, mkb, RI1, RCI = (cm[:, i, :] for i in range(6))
            P.add("act", lambda e: e.activation(out=DT1[:], in_=Mf, func=AF.Exp, scale=lgf), reads=["cm", "lg"], writes=["DT1"])
            P.add("act", lambda e: e.activation(out=DT2[:], in_=Mb, func=AF.Exp, scale=lgb), reads=["cm", "lg"], writes=["DT2"])
            P.add("dve", lambda e: e.scalar_tensor_tensor(out=DT1[:], in0=DT1[:], scalar=QSCALE, in1=mkf, op0=ALU.mult, op1=ALU.mult),
                  reads=["DT1", "cm"], writes=["DT1"])
            P.add("dve", lambda e: e.scalar_tensor_tensor(out=DT2[:], in0=DT2[:], scalar=QSCALE, in1=mkb, op0=ALU.mult, op1=ALU.mult),
                  reads=["DT2", "cm"], writes=["DT2"])
            P.add("dve", lambda e: e.tensor_tensor(DTm[:], DT1[:], DT2[:], ALU.add), reads=["DT1", "DT2"], writes=["DTm"])
            P.add("act", lambda e: e.activation(out=DFq[:], in_=RI1, func=AF.Exp, scale=lgf), reads=["cm", "lg"], writes=["DFq"])
            P.add("act", lambda e: e.activation(out=DBq[:], in_=RCI, func=AF.Exp, scale=lgb), reads=["cm", "lg"], writes=["DBq"])
            P.add("act", lambda e: e.activation(out=hk[:, 4:5], in_=cj[:, 0:1], func=AF.Exp, scale=lgf), reads=["cj", "lg"], writes=["kdf0"])
            P.add("act", lambda e: e.activation(out=hk[:, 5:6], in_=cj[:, 1:2], func=AF.Exp, scale=lgb), reads=["cj", "lg"], writes=["kdb0"])
            P.add("dve", lambda e: e.tensor_scalar(hk[:, 0:2], hk[:, 4:6], QSCALE, None, ALU.mult), reads=["kdf0", "kdb0"], writes=["kd"])
            P.add("act", lambda e: e.activation(out=gCf, in_=lgf, func=AF.Exp, scale=128.0), reads=["lg"], writes=["gCf"])
            P.add("act", lambda e: e.activation(out=gCb, in_=lgb, func=AF.Exp, scale=128.0), reads=["lg"], writes=["gCb"])

        def retention(h, s):
            kdf, kdb, gCf, gCb = hk[:, 0:1], hk[:, 1:2], hk[:, 2:3], hk[:, 3:4]
            q0v = QT[:, 0, :].rearrange("p (c i) -> p c i", i=128)
            qfv = QT[:, 1, :].rearrange("p (c i) -> p c i", i=128)
            qbv = KT[:, 1, :].rearrange("p (c i) -> p c i", i=128)
            allq = [("QT", 0, tb) for tb in range(8)]
            P.add("pool", lambda e: e.memset(Sst[0][:], 0.0), writes=[("Sst", 0)])
            P.add("dve", lambda e: e.tensor_tensor(qfv, q0v, DFq[:].unsqueeze(1).to_broadcast([128, NT, 128]), ALU.mult),
                  reads=allq + ["DFq"], writes=[("Qf", c) for c in range(NT)])
            P.add("pool", lambda e: e.tensor_tensor(qbv, q0v, DBq[:].unsqueeze(1).to_broadcast([128, NT, 128]), ALU.mult),
                  reads=allq + ["DBq"], writes=[("Qb", c) for c in range(NT)])
            ktr = [BANK[2][:, 0:128], BANK[3][:, 0:128]]
            st = {"cur": 0, "sb": 0}

            def dS_ap(c):
                return BANK[4 + c % 2][:, 0:256]

            def ktrans(c, kd):
                cs = slice(c * 128, (c + 1) * 128)
                kb = c % 2
                P.add("pe", lambda e: e.matmul(ktr[kb], lhsT=KT[:, 0, cs], rhs=ident[:], start=True, stop=True),
                      reads=[("KT", 0, c // 4), "ident"], writes=[bk(2 + kb)])
                P.add("dve", lambda e: e.tensor_scalar(Kfb[kb][:], ktr[kb], kd, None, ALU.mult),
                      reads=[bk(2 + kb), "kd"], writes=[("Kfb", kb)])
                P.add("pe", lambda e: e.matmul(dS_ap(c), lhsT=Kfb[kb][:], rhs=Vb[:, c, 0:256], start=True, stop=True),
                      reads=[("Kfb", kb), ("V", c)], writes=[bk(4 + c % 2)])

            def chain(c, gC, gkey):
                cur = st["cur"]
                nxt = 1 - cur
                P.add("dve", lambda e: e.scalar_tensor_tensor(
                    out=Sst[nxt][:], in0=Sst[cur][:], scalar=gC, in1=dS_ap(c), op0=ALU.mult, op1=ALU.add),
                    reads=[("Sst", cur), gkey, bk(4 + c % 2)], writes=[("Sst", nxt)])
                st["cur"] = nxt
                return nxt

            st["cur"] = 0
            ktrans(0, kdf)
            ktrans(1, kdf)
            for c in range(NT - 1):
                nxt = chain(c, gCf, "gCf")
                P.add("act", lambda e, nxt=nxt, c=c: e.activation(out=SF[:, c + 1, :], in_=Sst[nxt][:], func=AF.Copy),
                      reads=[("Sst", nxt)], writes=[("SF", c + 1)])
                if c + 2 <= NT - 2:
                    ktrans(c + 2, kdf)
            cur0 = st["cur"]
            P.add("pool", lambda e: e.memset(Sst[cur0][:], 0.0), writes=[("Sst", cur0)])

            def b1(c):
                cs = slice(c * 128, (c + 1) * 128)
                kb = c % 2
                P.add("pe", lambda e: e.matmul(BANK[kb][:, 0:128], lhsT=KT[:, 0, cs], rhs=QT[:, 0, cs], start=True, stop=True),
                      reads=[("KT", 0, c // 4), ("QT", 0, c // 4)], writes=[bk(kb)])
                P.add("dve", lambda e: e.tensor_tensor(PTr[kb][:], BANK[kb][:, 0:128], DTm[:], ALU.mult),
                      reads=[bk(kb), "DTm"], writes=[("PTr", kb)])
                if c > 0:
                    ktrans(c, kdb)

            def b2(c):
                cs = slice(c * 128, (c + 1) * 128)
                kb = c % 2
                ob = 6 + kb
                has_f, has_b = c > 0, c < NT - 1
                sbcur = st["sb"]
                P.add("pe", lambda e: e.matmul(
                    BANK[ob][:, 0:256], lhsT=PTr[kb][:], rhs=Vb[:, c, 0:256], start=True, stop=not (has_f or has_b)),
                    reads=[("PTr", kb), ("V", c)], writes=[bk(ob)])
                if has_f:
                    P.add("pe", lambda e: e.matmul(
                        BANK[ob][:, 0:256], lhsT=QT[:, 1, cs], rhs=SF[:, c, :], start=False, stop=not has_b),
                        reads=[("Qf", c), ("SF", c)], writes=[bk(ob)])
                if has_b:
                    P.add("pe", lambda e: e.matmul(
                        BANK[ob][:, 0:256], lhsT=KT[:, 1, cs], rhs=SBb[sbcur][:], start=False, stop=True),
                        reads=[("Qb", c), ("SBb", sbcur)], writes=[bk(ob)])
                if c > 0:
                    nxt = chain(c, gCb, "gCb")
                    sbn = 1 - sbcur
                    P.add("dve", lambda e: e.tensor_copy(SBb[sbn][:], Sst[nxt][:]),
                          reads=[("Sst", nxt)], writes=[("SBb", sbn)])
                    st["sb"] = sbn
                sl = kb
                ssq = small[:, 16 + sl * 8 + 3: 16 + sl * 8 + 4]
                lnv = small[:, 16 + sl * 8 + 4: 16 + sl * 8 + 5]
                rstd = small[:, 16 + sl * 8 + 5: 16 + sl * 8 + 6]
                P.add("act", lambda e: e.activation(out=junk2[:], in_=BANK[ob][:, 0:256], func=AF.Square, accum_out=ssq),
                      reads=[bk(ob)], writes=["junk2", ("essq", sl)])
                P.add("act", lambda e: e.activation(out=lnv, in_=ssq, func=AF.Ln, scale=1.0 / 256, bias=EPS5),
                      reads=[("essq", sl), "c1"], writes=[("elnv", sl)])
                P.add("act", lambda e: e.activation(out=rstd, in_=lnv, func=AF.Exp, scale=-0.5),
                      reads=[("elnv", sl)], writes=[("erstd", sl)])

            def b_og(c):
                sl = c % 2
                ob = 6 + sl
                o4 = og_cnt[0] % 4
                og_cnt[0] += 1
                rstd = small[:, 16 + sl * 8 + 5: 16 + sl * 8 + 6]
                P.add("dve", lambda e: e.scalar_tensor_tensor(
                    out=ogst[o4][:], in0=BANK[ob][:, 0:256], scalar=rstd, in1=GS[:, c, :], op0=ALU.mult, op1=ALU.mult),
                    reads=[bk(ob), ("erstd", sl), ("GS", c)], writes=[("ogst", o4)])
                P.add("sp", lambda e: e.dma_start(
                    out=og_d[s * 2 + 1, c * 128:(c + 1) * 128, h * 256:(h + 1) * 256], in_=ogst[o4][:]),
                    reads=[("ogst", o4)], writes=[("ogd", s, 1, c, h)], dma=f"ogst{o4}")

            b1(NT - 1)
            b1(NT - 2)
            for c in range(NT - 1, -1, -1):
                b2(c)
                if c + 1 <= NT - 1:
                    b_og(c + 1)
                if c - 2 >= 0:
                    b1(c - 2)
            b_og(0)

        def load_tables(src_d):
            for q in range(2):
                P.add("pool", lambda e, q=q: e.dma_start(out=TAB[:, q * 4096:(q + 1) * 4096], in_=src_d[:, q * 4096:(q + 1) * 4096]),
                      writes=[("tab", q)], dma=f"tab{q}")

        for s in range(NSEQ):
            if s > 0:
                P.new_epoch()
            P.barrier()
            if 0 in do_layers:
                load_tables(tab0_d)
                P.add("pool", lambda e: e.dma_start(out=Wh_flat, in_=win0_d[0, :, :]), writes=["Wh"], dma="Wh")
                P.add("pool", lambda e: e.memset(Vb[:, :, 256:258], 1.0), writes=["Vones"])
            phase0(s)
            P.barrier()
            if 0 in do_layers:
                for h in range(NH):
                    projection(0, h, s, skip_w=(h == 0))
                    if h == NH - 1:
                        prefetch_wout(0)
                    attention(h, s)
                outproj(0, s)
            if 1 in do_layers:
                load_tables(tab1_d)
                for h in range(NH):
                    ret_consts(h)
                    projection(1, h, s)
                    if h == NH - 1:
                        prefetch_wout(1)
                    retention(h, s)
                outproj(1, s)
        P.barrier()
        P.add("sp", lambda e: None)

        keys = P.semkeys()
        P.assign()
        sems = {k: E(nc.semaphore("s_" + "_".join(str(x) for x in k))) for k in keys}
        block = E(nc.Block())

        @block.tensor
        def _(e):
            P.emit_engine("pe", e, sems)

        @block.scalar
        def _(e):
            P.emit_engine("act", e, sems)

        @block.vector
        def _(e):
            P.emit_engine("dve", e, sems)

        @block.gpsimd
        def _(e):
            P.emit_engine("pool", e, sems)

        @block.sync
        def _(e):
            P.emit_engine("sp", e, sems)
    return nc, P


def _tables():
    t = np.arange(S, dtype=np.float32)
    inv0 = (10000.0 ** (-np.arange(0, 128, 2, dtype=np.float32) / 128)).astype(np.float32)
    ang0 = (t[:, None] * inv0[None, :]).astype(np.float32)
    inv1 = (1.0 / (10000.0 ** np.linspace(0.0, 1.0, 64, dtype=np.float32))).astype(np.float32)
    ang1 = (t[:, None] * inv1[None, :]).astype(np.float32)

    def mk(ang):
        c = np.cos(ang).astype(np.float32).T
        sn = np.sin(ang).astype(np.float32).T
        cos = np.concatenate([c, c], 0)
        sin = np.concatenate([-sn, sn], 0)
        return np.ascontiguousarray(np.concatenate([cos, sin], 1))
    return mk(ang0), mk(ang1)


def _cmask():
    i = np.arange(128, dtype=np.float32)
    jj = i[:, None]
    ii = i[None, :]
    Mf = np.maximum(ii - jj, 0.0)
    Mb = np.maximum(jj - ii, 0.0)
    mkf = (ii >= jj).astype(np.float32)
    mkb = (jj > ii).astype(np.float32)
    RI1 = np.broadcast_to(ii + 1.0, (128, 128))
    RCI = np.broadcast_to(128.0 - ii, (128, 128))
    cm = np.stack([Mf, Mb, mkf, mkb, RI1, RCI], 1).astype(np.float32)
    cj = np.stack([127.0 - i, i], 1).astype(np.float32)
    return np.ascontiguousarray(cm.reshape(128, 768)), np.ascontiguousarray(cj)


def _rep(v, n=128):
    v = np.asarray(v, dtype=np.float32).reshape(1, -1)
    return np.ascontiguousarray(np.broadcast_to(v, (n, v.shape[1])))


def prep_shared(norm_g, da_w_in, da_lambda_q1, da_lambda_k1, da_lambda_q2, da_lambda_k2, da_subln_g,
                da_w_out, ret_w_in, ret_decay_fwd, ret_decay_bwd, ret_subln_g, ret_w_out, final_norm_g):
    w0 = np.asarray(da_w_in[0], dtype=np.float32)
    w0 = w0.reshape(8, 128, 4, 8, 256)
    w_in0 = np.ascontiguousarray(w0.transpose(3, 1, 0, 2, 4)).reshape(8, 128, 8192)
    w1 = np.asarray(ret_w_in[0], dtype=np.float32)
    perm = np.concatenate([np.arange(0, 128, 2), np.arange(1, 128, 2)])
    q1 = w1[:, 0:1024].reshape(8, 128, 8, 128)[:, :, :, perm]
    k1 = w1[:, 1024:2048].reshape(8, 128, 8, 128)[:, :, :, perm]
    v1 = w1[:, 2048:4096].reshape(8, 128, 8, 256)
    g1w = w1[:, 4096:6144].reshape(8, 128, 8, 256)
    cat = np.concatenate([q1, k1, v1, g1w], axis=3)
    w_in1 = np.ascontiguousarray(cat.transpose(2, 1, 0, 3)).reshape(8, 128, 6144)

    def wout(w):
        w = np.asarray(w, dtype=np.float32).reshape(16, 128, 1024)
        return np.ascontiguousarray(w.transpose(1, 0, 2)).reshape(128, 16384)
    tab0, tab1 = _tables()
    cm, cj = _cmask()
    shared = {
        "w_in0": w_in0, "w_out0": wout(da_w_out[0]), "w_in1": w_in1, "w_out1": wout(ret_w_out[0]),
        "g0": _rep(norm_g[0]), "g1": _rep(norm_g[1]), "gf": _rep(final_norm_g),
        "sub": np.ascontiguousarray(np.concatenate([_rep(da_subln_g[0]), _rep(ret_subln_g[0])], 1)),
        "lam4": np.ascontiguousarray(np.concatenate([_rep(da_lambda_q1[0]), _rep(da_lambda_k1[0]),
                                                     _rep(da_lambda_q2[0]), _rep(da_lambda_k2[0])], 1)),
        "dec": np.ascontiguousarray(np.concatenate([_rep(ret_decay_fwd[0]), _rep(ret_decay_bwd[0])], 1)),
        "tab0": tab0, "tab1": tab1, "cmask": cm, "cj": cj,
    }
    return shared


_CACHE = {}


def kernel(x_prompt, x_sample, norm_g, da_w_in, da_lambda_q1, da_lambda_k1, da_lambda_q2,
           da_lambda_k2, da_subln_g, da_w_out, ret_w_in, ret_decay_fwd, ret_decay_bwd,
           ret_subln_g, ret_w_out, final_norm_g):
    x_prompt = np.asarray(x_prompt, dtype=np.float32)
    x_sample = np.asarray(x_sample, dtype=np.float32)
    shared = prep_shared(norm_g, da_w_in, da_lambda_q1, da_lambda_k1, da_lambda_q2, da_lambda_k2, da_subln_g,
                         da_w_out, ret_w_in, ret_decay_fwd, ret_decay_bwd, ret_subln_g, ret_w_out, final_norm_g)
    xs_all = np.concatenate([x_prompt, x_sample], 0)
    n = 8
    per = xs_all.shape[0] // n
    if "nc" not in _CACHE:
        _CACHE["nc"] = build(NSEQ=per)[0]
    nc = _CACHE["nc"]
    in_maps = []
    for c in range(n):
        m = dict(shared)
        m["x"] = np.ascontiguousarray(xs_all[c * per:(c + 1) * per])
        in_maps.append(m)
    res = run_bass_kernel_spmd(nc, in_maps, core_ids=list(range(n)))
    y = np.concatenate([np.asarray(r["y"], dtype=np.float32) for r in res.results], 0)
    nb = x_prompt.shape[0]
    return (np.ascontiguousarray(y[:nb]), np.ascontiguousarray(y[nb:]))
```

```python
import numpy as np
from contextlib import ExitStack
import concourse.bass as bass
import concourse.mybir as mybir
from concourse.bass_utils import run_bass_kernel_spmd

F32 = mybir.dt.float32
BF16 = mybir.dt.bfloat16
AF = mybir.ActivationFunctionType
ALU = mybir.AluOpType

S = 4096
D = 1024
NT = 32
NH = 8
ENGS = ("pe", "act", "dve", "pool", "sp")
QSCALE = 128.0 ** -0.5
LAMBDA_INIT0 = 0.8 - 0.6 * 1.0


class Op:
    __slots__ = ("eng", "fn", "deps", "signal", "semkey", "val", "is_dma", "idx")


class Planner:
    def __init__(self):
        self.ops = {e: [] for e in ENGS}
        self.epoch = 0
        self.last_w = {}
        self.readers = {}
        self.dma_last = {}
        self.dma_eng = {}
        self.bar_deps = []
        self.bar_need = {e: False for e in ENGS}

    def add(self, eng, fn, reads=(), writes=(), dma=None, extra=()):
        op = Op()
        op.eng = eng
        op.fn = fn
        op.is_dma = dma is not None
        op.signal = op.is_dma
        op.val = None
        op.semkey = ("dma", dma) if op.is_dma else (eng, self.epoch)
        if op.is_dma:
            assert self.dma_eng.setdefault(dma, eng) == eng
        deps = {}

        def dep(d, raw):
            if d is None:
                return
            if (not d.is_dma) and (not op.is_dma) and d.eng == eng:
                if eng == "pe" or (not raw and eng != "pool"):
                    return
            deps[id(d)] = d

        for r in reads:
            dep(self.last_w.get(r), True)
        for w in writes:
            dep(self.last_w.get(w), False)
            for rd in self.readers.get(w, ()):
                dep(rd, False)
        for d in extra:
            dep(d, True)
        if self.bar_need[eng]:
            for d in self.bar_deps:
                dep(d, True)
            self.bar_need[eng] = False
        best = {}
        for d in deps.values():
            b = best.get(d.semkey)
            if b is None or d.idx > b.idx:
                best[d.semkey] = d
        op.deps = list(best.values())
        for d in op.deps:
            d.signal = True
        op.idx = len(self.ops[eng])
        self.ops[eng].append(op)
        for r in reads:
            self.readers.setdefault(r, []).append(op)
        for w in writes:
            self.last_w[w] = op
            self.readers[w] = []
        if op.is_dma:
            self.dma_last[dma] = op
        return op

    def barrier(self):
        deps = []
        for e in ENGS:
            for op in reversed(self.ops[e]):
                if not op.is_dma:
                    deps.append(op)
                    break
        deps.extend(self.dma_last.values())
        self.bar_deps = deps
        for e in ENGS:
            self.bar_need[e] = True

    def new_epoch(self):
        self.epoch += 1

    def semkeys(self):
        keys = []
        seen = set()
        for e in ENGS:
            for op in self.ops[e]:
                if op.signal and op.semkey not in seen:
                    seen.add(op.semkey)
                    keys.append(op.semkey)
        return keys

    def assign(self):
        cnt = {}
        for e in ENGS:
            for op in self.ops[e]:
                if op.signal:
                    inc = 16 if op.is_dma else 1
                    cnt[op.semkey] = cnt.get(op.semkey, 0) + inc
                    op.val = cnt[op.semkey]
        return cnt

    def emit_engine(self, e, eng, sems):
        waited = {}
        for op in self.ops[e]:
            need = {}
            for d in op.deps:
                if d.val > need.get(d.semkey, 0):
                    need[d.semkey] = d.val
            for k, v in need.items():
                if waited.get(k, 0) < v:
                    eng.wait_ge(sems[k], v)
                    waited[k] = v
            inst = op.fn(eng)
            if op.signal and inst is not None:
                inst.then_inc(sems[op.semkey], 16 if op.is_dma else 1)


def build(NSEQ=3, do_layers=(0, 1), attn_qb=16, dbg=False):
    nc = bass.Bass("TRN2", target_bir_lowering=False)
    dt_in = lambda name, shape: nc.dram_tensor(name, shape, F32, kind="ExternalInput").ap()
    x_d = dt_in("x", [NSEQ, S, D])
    win0_d = dt_in("w_in0", [NH, 128, 8192])
    wout0_d = dt_in("w_out0", [128, 16384])
    win1_d = dt_in("w_in1", [NH, 128, 6144])
    wout1_d = dt_in("w_out1", [128, 16384])
    g0_d = dt_in("g0", [128, D])
    g1_d = dt_in("g1", [128, D])
    gf_d = dt_in("gf", [128, D])
    sub_d = dt_in("sub", [128, 512])
    lam_d = dt_in("lam4", [128, 512])
    dec_d = dt_in("dec", [128, 16])
    tab0_d = dt_in("tab0", [128, 8192])
    tab1_d = dt_in("tab1", [128, 8192])
    cm_d = dt_in("cmask", [128, 6 * 128])
    cj_d = dt_in("cj", [128, 2])
    y_d = nc.dram_tensor("y", [NSEQ, S, D], F32, kind="ExternalOutput").ap()
    og_d = nc.dram_tensor("og_scr", [NSEQ * 2, S, 2048], BF16).ap()
    x1_d = (nc.dram_tensor("x1_scr", [NSEQ, S, D], F32, kind="ExternalOutput").ap() if dbg
            else nc.dram_tensor("x1_scr", [NSEQ, S, D], F32).ap())

    P = Planner()
    with ExitStack() as es:
        E = es.enter_context
        hT = E(nc.sbuf_tensor("hT", [128, 8, S], BF16))
        ARENA = E(nc.sbuf_tensor("arena", [128, 49280], BF16))
        A = ARENA
        QT = A[:, 0:8192].rearrange("p (c t) -> p c t", c=2)
        KT = A[:, 8192:16384].rearrange("p (c t) -> p c t", c=2)
        Vb = A[:, 16384:16384 + 8256].rearrange("p (t e) -> p t e", e=258)
        GS = A[:, 24640:24640 + 8192].rearrange("p (t e) -> p t e", e=256)
        Wh = A[:, 32832:32832 + 8192].rearrange("p (c n) -> p c n", n=1024)
        Wh_flat = A[:, 32832:32832 + 8192]
        TAB = A[:, 41024:41024 + 8192]
        COS = A[:, 41024:41024 + 4096]
        SIN = A[:, 41024 + 4096:41024 + 8192]
        SF = None
        Wout = A[:, 32832:49216].rearrange("p (j n) -> p j n", n=1024)
        Wout_flat = A[:, 32832:49216]
        ogin = [A[:, 16384 + i * 2048:16384 + (i + 1) * 2048] for i in range(2)]
        ogT = [A[:, 20480 + i * 2048:20480 + (i + 1) * 2048].rearrange("p (j t) -> p j t", t=128) for i in range(2)]
        xs = [A[:, 24576 + i * 1024:24576 + (i + 1) * 1024] for i in range(2)]
        junk = A[:, 26624:26624 + 1024]
        xin = [A[:, 27648 + i * 2048:27648 + (i + 1) * 2048].bitcast(F32) for i in range(2)]
        x1t = [A[:, i * 2048:(i + 1) * 2048].bitcast(F32) for i in range(2)]
        grep_ = A[:, 4096:6144].bitcast(F32)
        yt = [A[:, 6144 + i * 2048:6144 + (i + 1) * 2048].bitcast(F32) for i in range(2)]

        SCR = E(nc.sbuf_tensor("scr", [128, 8192], BF16))
        PT = [SCR[:, i * 1024:(i + 1) * 1024] for i in range(3)]
        SF = SCR[:, 0:8192].rearrange("p (t e) -> p t e", e=256)
        t1b = [E(nc.sbuf_tensor(f"t1b{i}", [128, 512], F32)) for i in range(2)]
        t2b = [E(nc.sbuf_tensor(f"t2b{i}", [128, 512], F32)) for i in range(2)]
        sgb = [E(nc.sbuf_tensor(f"sgb{i}", [128, 256], F32)) for i in range(2)]
        oe = [SCR[:, 3072 + i * 1032:3072 + (i + 1) * 1032].bitcast(F32).rearrange("p (a b) -> p a b", a=2) for i in range(2)]
        ttb = [SCR[:, 5136 + i * 512:5136 + (i + 1) * 512].bitcast(F32) for i in range(2)]
        ocb = [SCR[:, 6160 + i * 512:6160 + (i + 1) * 512].bitcast(F32) for i in range(2)]
        ogst = [E(nc.sbuf_tensor(f"ogst{i}", [128, 256], BF16)) for i in range(4)]
        junk2 = E(nc.sbuf_tensor("junk2", [128, 256], BF16))
        junk3 = SCR[:, 7440:7696]
        small = E(nc.sbuf_tensor("small", [128, 64], F32))
        ident = E(nc.sbuf_tensor("ident", [128, 128], BF16))
        subg = E(nc.sbuf_tensor("subg", [128, 512], F32))
        lam4 = E(nc.sbuf_tensor("lam4s", [128, 512], F32))
        lamj = E(nc.sbuf_tensor("lamj", [128, 128], F32))
        dec = E(nc.sbuf_tensor("decs", [128, 16], F32))
        lg = E(nc.sbuf_tensor("lg", [128, 16], F32))
        cm = E(nc.sbuf_tensor("cm", [128, 6, 128], F32))
        cj = E(nc.sbuf_tensor("cjs", [128, 2], F32))
        consts = E(nc.sbuf_tensor("consts", [128, 16], F32))
        DT1 = E(nc.sbuf_tensor("DT1", [128, 128], F32))
        DT2 = E(nc.sbuf_tensor("DT2", [128, 128], F32))
        DTm = E(nc.sbuf_tensor("DTm", [128, 128], F32))
        DFq = E(nc.sbuf_tensor("DFq", [128, 128], BF16))
        DBq = E(nc.sbuf_tensor("DBq", [128, 128], BF16))
        hk = E(nc.sbuf_tensor("hk", [128, 8], F32))
        Kfb = [E(nc.sbuf_tensor(f"Kfb{i}", [128, 128], BF16)) for i in range(2)]
        PTr = [E(nc.sbuf_tensor(f"PTr{i}", [128, 128], BF16)) for i in range(2)]
        Sst = [E(nc.sbuf_tensor(f"Sst{i}", [128, 256], F32)) for i in range(2)]
        SBb = [E(nc.sbuf_tensor(f"SBb{i}", [128, 256], BF16)) for i in range(2)]
        PS = E(nc.psum_tensor("ps_all", [128, 4096], F32))
        BANK = [PS[:, i * 512:(i + 1) * 512] for i in range(8)]

        def sm(i):
            return small[:, i:i + 1]
        EPS6, EPS5, NEGLAM = consts[:, 0:1], consts[:, 1:2], consts[:, 2:3]

        def bk(i):
            return ("bank", i)

        P.add("pool", lambda e: e.memset(ident[:], 0.0), writes=["ident"])
        P.add("pool", lambda e: e.affine_select(out=ident[:], in_=ident[:], pattern=[[-1, 128]],
                                                compare_op=ALU.not_equal, fill=1.0, base=0,
                                                channel_multiplier=1), reads=["ident"], writes=["ident"])
        P.add("pool", lambda e: e.memset(consts[:, 0:1], 1e-6), writes=["c0"])
        P.add("pool", lambda e: e.memset(consts[:, 1:2], 1e-5), writes=["c1"])
        P.add("pool", lambda e: e.memset(small[:], 0.0), writes=["small"])
        for (dst, src, nm) in ((subg, sub_d, "sub"), (lam4, lam_d, "lam"), (dec, dec_d, "dec"),
                               (cj, cj_d, "cj")):
            P.add("sp", lambda e, dst=dst, src=src: e.dma_start(out=dst[:], in_=src[:, :]),
                  writes=[nm], dma="init_" + nm)
        P.add("sp", lambda e: e.dma_start(out=cm[:].rearrange("p a b -> p (a b)"), in_=cm_d[:, :]),
              writes=["cm"], dma="init_cm")
        P.add("dve", lambda e: e.tensor_scalar(subg[:, 0:256], subg[:, 0:256], 1.0 - LAMBDA_INIT0, None, ALU.mult),
              reads=["sub"], writes=["sub"])
        P.add("dve", lambda e: e.scalar_tensor_tensor(out=lamj[:], in0=lam4[:, 0:128], scalar=1.0, in1=lam4[:, 128:256],
                                                      op0=ALU.mult, op1=ALU.mult, accum_out=small[:, 60:61]),
              reads=["lam", "small"], writes=["lamj", "d1"])
        P.add("dve", lambda e: e.scalar_tensor_tensor(out=lamj[:], in0=lam4[:, 256:384], scalar=1.0, in1=lam4[:, 384:512],
                                                      op0=ALU.mult, op1=ALU.mult, accum_out=small[:, 61:62]),
              reads=["lam", "lamj", "small"], writes=["lamj", "d2"])
        P.add("act", lambda e: e.activation(out=small[:, 62:64], in_=small[:, 60:62], func=AF.Exp),
              reads=["d1", "d2"], writes=["e12"])
        P.add("dve", lambda e: e.tensor_tensor(consts[:, 2:3], small[:, 63:64], small[:, 62:63], ALU.subtract),
              reads=["e12"], writes=["neglam0"])
        P.add("dve", lambda e: e.tensor_scalar(consts[:, 2:3], consts[:, 2:3], -LAMBDA_INIT0, None, ALU.add),
              reads=["neglam0"], writes=["neglam"])
        P.add("act", lambda e: e.activation(out=lg[:], in_=dec[:], func=AF.Exp), reads=["dec"], writes=["lg0"])
        P.add("dve", lambda e: e.tensor_scalar(lg[:], lg[:], -1.0, None, ALU.mult), reads=["lg0"], writes=["lg"])

        def norm_tile(src_ap, src_key, t, slot, hbank):
            ssq, lnv, rstd = sm(slot * 4 + 0), sm(slot * 4 + 1), sm(slot * 4 + 2)
            P.add("act", lambda e: e.activation(out=junk, in_=src_ap, func=AF.Square, accum_out=ssq),
                  reads=[src_key], writes=["junk", ("ssq", slot)])
            P.add("act", lambda e: e.activation(out=lnv, in_=ssq, func=AF.Ln, scale=1.0 / D, bias=EPS6),
                  reads=[("ssq", slot), "c0"], writes=[("lnv", slot)])
            P.add("act", lambda e: e.activation(out=rstd, in_=lnv, func=AF.Exp, scale=-0.5),
                  reads=[("lnv", slot)], writes=[("rstd", slot)])
            P.add("dve", lambda e: e.scalar_tensor_tensor(out=xs[slot], in0=src_ap, scalar=rstd, in1=grep_,
                                                          op0=ALU.mult, op1=ALU.mult),
                  reads=[src_key, ("rstd", slot), "grep"], writes=[("xs", slot)])
            b0 = 4 + 2 * slot
            bv = PS[:, b0 * 512:(b0 + 2) * 512].rearrange("p (c t) -> p c t", c=8)
            for c in range(8):
                P.add("pe", lambda e, c=c: e.matmul(bv[:, c, :], lhsT=xs[slot][:, c * 128:(c + 1) * 128], rhs=ident[:], start=True, stop=True),
                      reads=[("xs", slot), "ident"], writes=[bk(b0 + c // 4)])
            P.add("act", lambda e: e.activation(out=hT[:, :, t * 128:(t + 1) * 128], in_=bv, func=AF.Copy),
                  reads=[bk(b0), bk(b0 + 1)], writes=[("hT", t)])

        def load_gain(src_d):
            P.add("sp", lambda e: e.dma_start(out=grep_, in_=src_d[:, :]), writes=["grep"], dma="grep")

        def phase0(s):
            xin4 = [A[:, i * 2048:(i + 1) * 2048].bitcast(F32) for i in range(4)]
            xs4 = [A[:, 8192 + i * 1024:8192 + (i + 1) * 1024] for i in range(4)]
            junk = A[:, 12288:13312]
            grep_ = A[:, 13312:15360].bitcast(F32)
            P.add("sp", lambda e: e.dma_start(out=grep_, in_=g0_d[:, :]), writes=["grepp0"], dma="grepp0")

            def L(t):
                k = t % 4
                P.add("sp", lambda e: e.dma_start(out=xin4[k], in_=x_d[s, t * 128:(t + 1) * 128, :]),
                      writes=[("xin4", k)], dma=f"xin4_{k}")

            def N1(t):
                k = t % 4
                ssq, lnv, rstd = sm(k * 4 + 0), sm(k * 4 + 1), sm(k * 4 + 2)
                P.add("act", lambda e: e.activation(out=junk, in_=xin4[k], func=AF.Square, accum_out=ssq),
                      reads=[("xin4", k)], writes=["junk", ("ssq4", k)])
                P.add("act", lambda e: e.activation(out=lnv, in_=ssq, func=AF.Ln, scale=1.0 / D, bias=EPS6),
                      reads=[("ssq4", k), "c0"], writes=[("lnv4", k)])
                P.add("act", lambda e: e.activation(out=rstd, in_=lnv, func=AF.Exp, scale=-0.5),
                      reads=[("lnv4", k)], writes=[("rstd4", k)])
                P.add("dve", lambda e: e.scalar_tensor_tensor(out=xs4[k], in0=xin4[k], scalar=rstd, in1=grep_,
                                                              op0=ALU.mult, op1=ALU.mult),
                      reads=[("xin4", k), ("rstd4", k), "grepp0"], writes=[("xs4", k)])

            def N2(t):
                k = t % 4
                b0 = 2 * k
                bv = PS[:, b0 * 512:(b0 + 2) * 512].rearrange("p (c t) -> p c t", c=8)
                for c in range(8):
                    P.add("pe", lambda e, c=c: e.matmul(bv[:, c, :], lhsT=xs4[k][:, c * 128:(c + 1) * 128], rhs=ident[:], start=True, stop=True),
                          reads=[("xs4", k), "ident"], writes=[bk(b0 + c // 4)])
                P.add("act", lambda e: e.activation(out=hT[:, :, t * 128:(t + 1) * 128], in_=bv, func=AF.Copy),
                      reads=[bk(b0), bk(b0 + 1)], writes=[("hT", t)])

            for t in range(3):
                L(t)
            for t in range(NT):
                if t + 3 < NT:
                    L(t + 3)
                N1(t)
                if t >= 1:
                    N2(t - 1)
            N2(NT - 1)

        def rope_evac(bank, dst, tb, slot):
            ps = BANK[bank]
            cs = slice(tb * 512, (tb + 1) * 512)
            P.add("dve", lambda e: e.tensor_tensor(t1b[slot][:], ps[:], COS[:, cs], ALU.mult),
                  reads=[bk(bank), ("tab", 0), ("tab", 1)], writes=[("t1", slot)])
            P.add("dve", lambda e: e.tensor_tensor(t2b[slot][0:64, :], ps[64:128, :], SIN[0:64, cs], ALU.mult),
                  reads=[bk(bank), ("tab", 0), ("tab", 1)], writes=[("t2a", slot)])
            P.add("dve", lambda e: e.tensor_tensor(t2b[slot][64:128, :], ps[0:64, :], SIN[64:128, cs], ALU.mult),
                  reads=[bk(bank), ("tab", 0), ("tab", 1)], writes=[("t2b", slot)])
            P.add("pool", lambda e: e.tensor_tensor(dst, t1b[slot][:], t2b[slot][:], ALU.add),
                  reads=[("t1", slot), ("t2a", slot), ("t2b", slot)], writes=[])
            return

        def projection(layer, h, s, skip_w=False):
            if layer == 0:
                if not skip_w:
                    P.add("pool", lambda e: e.dma_start(out=Wh_flat, in_=win0_d[h, :, :]),
                          writes=["Wh"], dma="Wh")
                nrow, vg0 = 4, 512
            else:
                if not skip_w:
                    P.add("pool", lambda e: e.dma_start(out=Wh[:, :, 0:768],
                                                        in_=win1_d[h, :, :].rearrange("p (c n) -> p c n", n=768)),
                          writes=["Wh"], dma="Wh")
                nrow, vg0 = 2, 256
            gsub = subg[:, 0:256] if layer == 0 else subg[:, 256:512]
            cnt = {"q": 0, "v": 0}

            def qk_block(r, tb):
                bank = cnt["q"] % 2
                slot = cnt["q"] % 2
                cnt["q"] += 1
                for c in range(8):
                    P.add("pe", lambda e, c=c: e.matmul(
                        BANK[bank][:], lhsT=Wh[:, c, r * 128:(r + 1) * 128], rhs=hT[:, c, tb * 512:(tb + 1) * 512],
                        start=(c == 0), stop=(c == 7)),
                        reads=["Wh"] + [("hT", tb * 4 + i) for i in range(4)], writes=[bk(bank)])
                if layer == 0:
                    dst_t, sub = (QT, r) if r < 2 else (KT, r - 2)
                else:
                    dst_t, sub = (QT, 0) if r == 0 else (KT, 0)
                dkey = ("QT" if dst_t is QT else "KT", sub, tb)
                dst = dst_t[:, sub, tb * 512:(tb + 1) * 512]
                ps = BANK[bank]
                cs = slice(tb * 512, (tb + 1) * 512)
                tabk = [("tab", 0), ("tab", 1)]
                P.add("dve", lambda e: e.tensor_tensor(t1b[slot][:], ps[:], COS[:, cs], ALU.mult),
                      reads=[bk(bank)] + tabk, writes=[("t1", slot)])
                P.add("dve", lambda e: e.tensor_tensor(t2b[slot][0:64, :], ps[64:128, :], SIN[0:64, cs], ALU.mult),
                      reads=[bk(bank)] + tabk, writes=[("t2a", slot)])
                P.add("dve", lambda e: e.tensor_tensor(t2b[slot][64:128, :], ps[0:64, :], SIN[64:128, cs], ALU.mult),
                      reads=[bk(bank)] + tabk, writes=[("t2b", slot)])
                P.add("pool", lambda e: e.tensor_tensor(dst, t1b[slot][:], t2b[slot][:], ALU.add),
                      reads=[("t1", slot), ("t2a", slot), ("t2b", slot)], writes=[dkey])

            def vg_tile(t):
                bank = 2 + cnt["v"] % 2
                slot = cnt["v"] % 2
                cnt["v"] += 1
                for c in range(8):
                    P.add("pe", lambda e, c=c: e.matmul(
                        BANK[bank][:], lhsT=hT[:, c, t * 128:(t + 1) * 128], rhs=Wh[:, c, vg0:vg0 + 512],
                        start=(c == 0), stop=(c == 7)),
                        reads=["Wh", ("hT", t)], writes=[bk(bank)])
                P.add("act", lambda e: e.activation(out=Vb[:, t, 0:256], in_=BANK[bank][:, 0:256], func=AF.Copy),
                      reads=[bk(bank)], writes=[("V", t)])
                P.add("act", lambda e: e.activation(out=sgb[slot][:], in_=BANK[bank][:, 256:512], func=AF.Silu),
                      reads=[bk(bank)], writes=[("sg", slot)])
                P.add("pool", lambda e: e.tensor_tensor(GS[:, t, :], sgb[slot][:], gsub, ALU.mult),
                      reads=[("sg", slot), "sub"], writes=[("GS", t)])

            qk_list = [(r, tb) for r in range(nrow) for tb in range(8)]
            per = NT // len(qk_list)
            t = 0
            for (r, tb) in qk_list:
                qk_block(r, tb)
                for _ in range(per):
                    vg_tile(t)
                    t += 1

        og_cnt = [0]

        def attention(h, s):
            NQB = attn_qb
            nsteps = NQB * NT

            def QK(i):
                qb, kt = divmod(i, NT)
                b = i % 4
                for sh in range(2):
                    P.add("pe", lambda e, sh=sh, qb=qb, kt=kt, b=b: e.matmul(
                        BANK[b][:, sh * 256:(sh + 1) * 256], lhsT=KT[:, sh, kt * 128:(kt + 1) * 128],
                        rhs=QT[:, sh, qb * 256:(qb + 1) * 256], start=True, stop=True),
                        reads=[("KT", sh, kt // 4), ("QT", sh, qb // 2)], writes=[bk(b)])

            def EXPG(j):
                g = j % 2
                p = j % 3
                P.add("act", lambda e, g=g, p=p: e.activation(out=PT[p][:], in_=PS[:, g * 1024:(g + 1) * 1024], func=AF.Exp, scale=QSCALE),
                      reads=[bk(2 * g), bk(2 * g + 1)], writes=[("PT", p)])

            def PV(i):
                qb, kt = divmod(i, NT)
                p = (i // 2) % 3
                off = (i % 2) * 512
                for sh in range(2):
                    for qt in range(2):
                        a = 4 + sh * 2 + qt
                        P.add("pe", lambda e, sh=sh, qt=qt, a=a, kt=kt, p=p, off=off: e.matmul(
                            BANK[a][:, 0:257], lhsT=PT[p][:, off + sh * 256 + qt * 128: off + sh * 256 + (qt + 1) * 128],
                            rhs=Vb[:, kt, 0:257], start=(kt == 0), stop=(kt == NT - 1)),
                            reads=[("PT", p), ("V", kt), "Vones"], writes=[bk(a)])

            def EPI_EVAC(qb):
                for sh in range(2):
                    for qt in range(2):
                        a = 4 + sh * 2 + qt
                        sl = (qb * 2 + qt) % 2
                        if sh == 0:
                            P.add("dve", lambda e, sl=sl, a=a, sh=sh: e.tensor_copy(oe[sl][:, sh, 0:257], BANK[a][:, 0:257]),
                                  reads=[bk(a)], writes=[("oe%d" % sh, sl)])
                        else:
                            P.add("act", lambda e, sl=sl, a=a, sh=sh: e.activation(out=oe[sl][:, sh, 0:257], in_=BANK[a][:, 0:257], func=AF.Copy),
                                  reads=[bk(a)], writes=[("oe%d" % sh, sl)])

            def EPI_A(qb, qt):
                tile = qb * 2 + qt
                sl = tile % 2
                rs = small[:, 16 + sl * 8: 16 + sl * 8 + 2]
                nl1 = small[:, 16 + sl * 8 + 2: 16 + sl * 8 + 3]
                P.add("dve", lambda e, sl=sl, rs=rs: e.reciprocal(rs, oe[sl][:, :, 256]),
                      reads=[("oe0", sl), ("oe1", sl)], writes=[("rs", sl)])
                P.add("dve", lambda e, rs=rs, nl1=nl1: e.tensor_tensor(nl1, rs[:, 1:2], NEGLAM, ALU.mult),
                      reads=[("rs", sl), "neglam"], writes=[("nl1", sl)])
                P.add("dve", lambda e, sl=sl, nl1=nl1: e.tensor_scalar(ttb[sl][:], oe[sl][:, 1, 0:256], nl1, None, ALU.mult),
                      reads=[("oe1", sl), ("nl1", sl)], writes=[("tt", sl)])
                P.add("dve", lambda e, sl=sl, rs=rs: e.scalar_tensor_tensor(
                    out=ocb[sl][:], in0=oe[sl][:, 0, 0:256], scalar=rs[:, 0:1], in1=ttb[sl][:],
                    op0=ALU.mult, op1=ALU.add),
                    reads=[("oe0", sl), ("rs", sl), ("tt", sl)], writes=[("oc", sl)])
                ssq = small[:, 40 + qt: 41 + qt]
                P.add("dve", lambda e, sl=sl, ssq=ssq: e.scalar_tensor_tensor(
                    out=junk3[:], in0=ocb[sl][:], scalar=1.0, in1=ocb[sl][:], op0=ALU.mult, op1=ALU.mult, accum_out=ssq),
                    reads=[("oc", sl)], writes=["junk3", ("essq2", qt)])

            def EPI_B(qb):
                P.add("act", lambda e: e.activation(out=small[:, 42:44], in_=small[:, 40:42], func=AF.Ln, scale=1.0 / 256, bias=EPS5),
                      reads=[("essq2", 0), ("essq2", 1), "c1"], writes=["elnv2"])
                P.add("act", lambda e: e.activation(out=small[:, 44:46], in_=small[:, 42:44], func=AF.Exp, scale=-0.5),
                      reads=["elnv2"], writes=["erstd2"])
                for qt in range(2):
                    tile = qb * 2 + qt
                    sl = tile % 2
                    o4 = og_cnt[0] % 4
                    og_cnt[0] += 1
                    rstd = small[:, 44 + qt: 45 + qt]
                    P.add("dve", lambda e, sl=sl, rstd=rstd, tile=tile, o4=o4: e.scalar_tensor_tensor(
                        out=ogst[o4][:], in0=ocb[sl][:], scalar=rstd, in1=GS[:, tile, :], op0=ALU.mult, op1=ALU.mult),
                        reads=[("oc", sl), "erstd2", ("GS", tile)], writes=[("ogst", o4)])
                    P.add("sp", lambda e, tile=tile, o4=o4: e.dma_start(
                        out=og_d[s * 2 + 0, tile * 128:(tile + 1) * 128, h * 256:(h + 1) * 256], in_=ogst[o4][:]),
                        reads=[("ogst", o4)], writes=[("ogd", s, 0, tile, h)], dma=f"ogst{o4}")

            pending = []

            npairs = nsteps // 2
            for i in range(4):
                QK(i)
            EXPG(0)
            for j in range(npairs):
                if j + 1 < npairs:
                    EXPG(j + 1)
                for i in (2 * j + 4, 2 * j + 5):
                    if i < nsteps:
                        QK(i)
                PV(2 * j)
                PV(2 * j + 1)
                while pending and pending[0][0] <= j:
                    pending.pop(0)[1]()
                if (2 * j + 1) % NT == NT - 1:
                    qb = (2 * j + 1) // NT
                    EPI_EVAC(qb)
                    EPI_A(qb, 0)
                    EPI_A(qb, 1)
                    pending.append((j + 6, lambda qb=qb: EPI_B(qb)))
            while pending:
                pending.pop(0)[1]()

        def prefetch_wout(layer):
            wsrc = wout0_d if layer == 0 else wout1_d
            alias = {0: ["Wh"], 1: ["Wh"], 2: [("tab", 0)], 3: [("tab", 1)]}
            for q in range(4):
                P.add("pool", lambda e, q=q: e.dma_start(out=Wout_flat[:, q * 4096:(q + 1) * 4096],
                                                         in_=wsrc[:, q * 4096:(q + 1) * 4096]),
                      writes=[("Wout", q)] + alias[q], dma=f"Wout{q}")

        def outproj(layer, s):
            P.barrier()
            load_gain(g1_d if layer == 0 else gf_d)
            tpv = PS[:, 0:2048].rearrange("p (c t) -> p c t", c=16)
            xsrc = x_d if layer == 0 else x1_d

            def stA(t):
                slot = t % 2
                rows = slice(t * 128, (t + 1) * 128)
                P.add("sp", lambda e: e.dma_start(out=ogin[slot], in_=og_d[s * 2 + layer, rows, :]),
                      reads=[("ogd", s, layer, t, hh) for hh in range(NH)], writes=[("ogin", slot)], dma=f"ogin{slot}")
                P.add("sp", lambda e: e.dma_start(out=xin[slot], in_=xsrc[s, rows, :]),
                      reads=[("x1d", s, t)], writes=[("xin", slot)], dma=f"xin{slot}")
                for j in range(16):
                    P.add("pe", lambda e, j=j: e.matmul(tpv[:, j, :], lhsT=ogin[slot][:, j * 128:(j + 1) * 128], rhs=ident[:], start=True, stop=True),
                          reads=[("ogin", slot), "ident"], writes=[bk(j // 4)])
                P.add("act", lambda e: e.activation(out=ogT[slot][:, 0:8, :], in_=tpv[:, 0:8, :], func=AF.Copy),
                      reads=[bk(0), bk(1)], writes=[("ogTa", slot)])
                P.add("dve", lambda e: e.tensor_copy(ogT[slot][:, 8:16, :], tpv[:, 8:16, :]),
                      reads=[bk(2), bk(3)], writes=[("ogTb", slot)])

            def stB(t):
                slot = t % 2
                rows = slice(t * 128, (t + 1) * 128)
                for half in range(2):
                    for j in range(16):
                        P.add("pe", lambda e, j=j, half=half: e.matmul(
                            BANK[4 + half][:], lhsT=ogT[slot][:, j, :], rhs=Wout[:, j, half * 512:(half + 1) * 512],
                            start=(j == 0), stop=(j == 15)),
                            reads=[("ogTa", slot), ("ogTb", slot), ("Wout", j // 4)], writes=[bk(4 + half)])
                for half in range(2):
                    hs = slice(half * 512, (half + 1) * 512)
                    P.add("dve", lambda e, half=half, hs=hs: e.tensor_tensor(x1t[slot][:, hs], BANK[4 + half][:], xin[slot][:, hs], ALU.add),
                          reads=[bk(4 + half), ("xin", slot)], writes=[("x1t", slot, half)])
                src_keys = [("x1t", slot, 0), ("x1t", slot, 1)]
                ssq, lnv, rstd = sm(slot * 4 + 0), sm(slot * 4 + 1), sm(slot * 4 + 2)
                if layer == 0:
                    P.add("pool", lambda e: e.dma_start(out=x1_d[s, rows, :], in_=x1t[slot]),
                          reads=src_keys, writes=[("x1d", s, t)], dma=f"x1st{slot}")
                P.add("act", lambda e: e.activation(out=junk, in_=x1t[slot], func=AF.Square, accum_out=ssq),
                      reads=src_keys, writes=["junk", ("ssq", slot)])
                P.add("act", lambda e: e.activation(out=lnv, in_=ssq, func=AF.Ln, scale=1.0 / D, bias=EPS6),
                      reads=[("ssq", slot), "c0"], writes=[("lnv", slot)])
                P.add("act", lambda e: e.activation(out=rstd, in_=lnv, func=AF.Exp, scale=-0.5),
                      reads=[("lnv", slot)], writes=[("rstd", slot)])
                if layer == 0:
                    P.add("dve", lambda e: e.scalar_tensor_tensor(
                        out=xs[slot], in0=x1t[slot], scalar=rstd, in1=grep_, op0=ALU.mult, op1=ALU.mult),
                        reads=src_keys + [("rstd", slot), "grep"], writes=[("xs", slot)])
                else:
                    P.add("dve", lambda e: e.scalar_tensor_tensor(
                        out=yt[slot], in0=x1t[slot], scalar=rstd, in1=grep_, op0=ALU.mult, op1=ALU.mult),
                        reads=src_keys + [("rstd", slot), "grep"], writes=[("yt", slot)])
                    P.add("pool", lambda e: e.dma_start(out=y_d[s, rows, :], in_=yt[slot]),
                          reads=[("yt", slot)], writes=[("yd", s, t)], dma=f"yst{slot}")

            def stC(t):
                slot = t % 2
                bv = PS[:, 3072:4096].rearrange("p (c t) -> p c t", c=8)
                for c in range(8):
                    P.add("pe", lambda e, c=c: e.matmul(bv[:, c, :], lhsT=xs[slot][:, c * 128:(c + 1) * 128], rhs=ident[:], start=True, stop=True),
                          reads=[("xs", slot), "ident"], writes=[bk(6 + c // 4)])
                P.add("act", lambda e: e.activation(out=hT[:, :, t * 128:(t + 1) * 128], in_=bv, func=AF.Copy),
                      reads=[bk(6), bk(7)], writes=[("hT", t)])

            stA(0)
            for t in range(NT):
                if t + 1 < NT:
                    stA(t + 1)
                stB(t)
                if layer == 0 and t >= 1:
                    stC(t - 1)
            if layer == 0:
                stC(NT - 1)
            P.barrier()

        def ret_consts(h):
            lgf, lgb = lg[:, h:h + 1], lg[:, 8 + h:9 + h]
            kdf, kdb, gCf, gCb = hk[:, 0:1], hk[:, 1:2], hk[:, 2:3], hk[:, 3:4]
            Mf, Mb, mkf, mkb, RI1, RCI = (cm[:, i, :] for i in range(6))
            P.add("act", lambda e: e.activation(out=DT1[:], in_=Mf, func=AF.Exp, scale=lgf), reads=["cm", "lg"], writes=["DT1"])
            P.add("act", lambda e: e.activation(out=DT2[:], in_=Mb, func=AF.Exp, scale=lgb), reads=["cm", "lg"], writes=["DT2"])
            P.add("dve", lambda e: e.scalar_tensor_tensor(out=DT1[:], in0=DT1[:], scalar=QSCALE, in1=mkf, op0=ALU.mult, op1=ALU.mult),
                  reads=["DT1", "cm"], writes=["DT1"])
            P.add("dve", lambda e: e.scalar_tensor_tensor(out=DT2[:], in0=DT2[:], scalar=QSCALE, in1=mkb, op0=ALU.mult, op1=ALU.mult),
                  reads=["DT2", "cm"], writes=["DT2"])
            P.add("dve", lambda e: e.tensor_tensor(DTm[:], DT1[:], DT2[:], ALU.add), reads=["DT1", "DT2"], writes=["DTm"])
            P.add("act", lambda e: e.activation(out=DFq[:], in_=RI1, func=AF.Exp, scale=lgf), reads=["cm", "lg"], writes=["DFq"])
            P.add("act", lambda e: e.activation(out=DBq[:], in_=RCI, func=AF.Exp, scale=lgb), reads=["cm", "lg"], writes=["DBq"])
            P.add("act", lambda e: e.activation(out=hk[:, 4:5], in_=cj[:, 0:1], func=AF.Exp, scale=lgf), reads=["cj", "lg"], writes=["kdf0"])
            P.add("act", lambda e: e.activation(out=hk[:, 5:6], in_=cj[:, 1:2], func=AF.Exp, scale=lgb), reads=["cj", "lg"], writes=["kdb0"])
            P.add("dve", lambda e: e.tensor_scalar(hk[:, 0:2], hk[:, 4:6], QSCALE, None, ALU.mult), reads=["kdf0", "kdb0"], writes=["kd"])
            P.add("act", lambda e: e.activation(out=gCf, in_=lgf, func=AF.Exp, scale=128.0), reads=["lg"], writes=["gCf"])
            P.add("act", lambda e: e.activation(out=gCb, in_=lgb, func=AF.Exp, scale=128.0), reads=["lg"], writes=["gCb"])

        def retention(h, s):
            kdf, kdb, gCf, gCb = hk[:, 0:1], hk[:, 1:2], hk[:, 2:3], hk[:, 3:4]
            q0v = QT[:, 0, :].rearrange("p (c i) -> p c i", i=128)
            qfv = QT[:, 1, :].rearrange("p (c i) -> p c i", i=128)
            qbv = KT[:, 1, :].rearrange("p (c i) -> p c i", i=128)
            allq = [("QT", 0, tb) for tb in range(8)]
            P.add("pool", lambda e: e.memset(Sst[0][:], 0.0), writes=[("Sst", 0)])
            P.add("dve", lambda e: e.tensor_tensor(qfv, q0v, DFq[:].unsqueeze(1).to_broadcast([128, NT, 128]), ALU.mult),
                  reads=allq + ["DFq"], writes=[("Qf", c) for c in range(NT)])
            P.add("pool", lambda e: e.tensor_tensor(qbv, q0v, DBq[:].unsqueeze(1).to_broadcast([128, NT, 128]), ALU.mult),
                  reads=allq + ["DBq"], writes=[("Qb", c) for c in range(NT)])
            ktr = [BANK[2][:, 0:128], BANK[3][:, 0:128]]
            st = {"cur": 0, "sb": 0}

            def dS_ap(c):
                return BANK[4 + c % 2][:, 0:256]

            def ktrans(c, kd):
                cs = slice(c * 128, (c + 1) * 128)
                kb = c % 2
                P.add("pe", lambda e: e.matmul(ktr[kb], lhsT=KT[:, 0, cs], rhs=ident[:], start=True, stop=True),
                      reads=[("KT", 0, c // 4), "ident"], writes=[bk(2 + kb)])
                P.add("dve", lambda e: e.tensor_scalar(Kfb[kb][:], ktr[kb], kd, None, ALU.mult),
                      reads=[bk(2 + kb), "kd"], writes=[("Kfb", kb)])
                P.add("pe", lambda e: e.matmul(dS_ap(c), lhsT=Kfb[kb][:], rhs=Vb[:, c, 0:256], start=True, stop=True),
                      reads=[("Kfb", kb), ("V", c)], writes=[bk(4 + c % 2)])

            def chain(c, gC, gkey):
                cur = st["cur"]
                nxt = 1 - cur
                P.add("dve", lambda e: e.scalar_tensor_tensor(
                    out=Sst[nxt][:], in0=Sst[cur][:], scalar=gC, in1=dS_ap(c), op0=ALU.mult, op1=ALU.add),
                    reads=[("Sst", cur), gkey, bk(4 + c % 2)], writes=[("Sst", nxt)])
                st["cur"] = nxt
                return nxt

            st["cur"] = 0
            ktrans(0, kdf)
            ktrans(1, kdf)
            for c in range(NT - 1):
                nxt = chain(c, gCf, "gCf")
                P.add("act", lambda e, nxt=nxt, c=c: e.activation(out=SF[:, c + 1, :], in_=Sst[nxt][:], func=AF.Copy),
                      reads=[("Sst", nxt)], writes=[("SF", c + 1)])
                if c + 2 <= NT - 2:
                    ktrans(c + 2, kdf)
            cur0 = st["cur"]
            P.add("pool", lambda e: e.memset(Sst[cur0][:], 0.0), writes=[("Sst", cur0)])

            def b1(c):
                cs = slice(c * 128, (c + 1) * 128)
                kb = c % 2
                P.add("pe", lambda e: e.matmul(BANK[kb][:, 0:128], lhsT=KT[:, 0, cs], rhs=QT[:, 0, cs], start=True, stop=True),
                      reads=[("KT", 0, c // 4), ("QT", 0, c // 4)], writes=[bk(kb)])
                P.add("dve", lambda e: e.tensor_tensor(PTr[kb][:], BANK[kb][:, 0:128], DTm[:], ALU.mult),
                      reads=[bk(kb), "DTm"], writes=[("PTr", kb)])
                if c > 0:
                    ktrans(c, kdb)

            def b2(c):
                cs = slice(c * 128, (c + 1) * 128)
                kb = c % 2
                ob = 6 + kb
                has_f, has_b = c > 0, c < NT - 1
                sbcur = st["sb"]
                P.add("pe", lambda e: e.matmul(
                    BANK[ob][:, 0:256], lhsT=PTr[kb][:], rhs=Vb[:, c, 0:256], start=True, stop=not (has_f or has_b)),
                    reads=[("PTr", kb), ("V", c)], writes=[bk(ob)])
                if has_f:
                    P.add("pe", lambda e: e.matmul(
                        BANK[ob][:, 0:256], lhsT=QT[:, 1, cs], rhs=SF[:, c, :], start=False, stop=not has_b),
                        reads=[("Qf", c), ("SF", c)], writes=[bk(ob)])
                if has_b:
                    P.add("pe", lambda e: e.matmul(
                        BANK[ob][:, 0:256], lhsT=KT[:, 1, cs], rhs=SBb[sbcur][:], start=False, stop=True),
                        reads=[("Qb", c), ("SBb", sbcur)], writes=[bk(ob)])
                if c > 0:
                    nxt = chain(c, gCb, "gCb")
                    sbn = 1 - sbcur
                    P.add("dve", lambda e: e.tensor_copy(SBb[sbn][:], Sst[nxt][:]),
                          reads=[("Sst", nxt)], writes=[("SBb", sbn)])
                    st["sb"] = sbn
                sl = kb
                ssq = small[:, 16 + sl * 8 + 3: 16 + sl * 8 + 4]
                lnv = small[:, 16 + sl * 8 + 4: 16 + sl * 8 + 5]
                rstd = small[:, 16 + sl * 8 + 5: 16 + sl * 8 + 6]
                P.add("act", lambda e: e.activation(out=junk2[:], in_=BANK[ob][:, 0:256], func=AF.Square, accum_out=ssq),
                      reads=[bk(ob)], writes=["junk2", ("essq", sl)])
                P.add("act", lambda e: e.activation(out=lnv, in_=ssq, func=AF.Ln, scale=1.0 / 256, bias=EPS5),
                      reads=[("essq", sl), "c1"], writes=[("elnv", sl)])
                P.add("act", lambda e: e.activation(out=rstd, in_=lnv, func=AF.Exp, scale=-0.5),
                      reads=[("elnv", sl)], writes=[("erstd", sl)])

            def b_og(c):
                sl = c % 2
                ob = 6 + sl
                o4 = og_cnt[0] % 4
                og_cnt[0] += 1
                rstd = small[:, 16 + sl * 8 + 5: 16 + sl * 8 + 6]
                P.add("dve", lambda e: e.scalar_tensor_tensor(
                    out=ogst[o4][:], in0=BANK[ob][:, 0:256], scalar=rstd, in1=GS[:, c, :], op0=ALU.mult, op1=ALU.mult),
                    reads=[bk(ob), ("erstd", sl), ("GS", c)], writes=[("ogst", o4)])
                P.add("sp", lambda e: e.dma_start(
                    out=og_d[s * 2 + 1, c * 128:(c + 1) * 128, h * 256:(h + 1) * 256], in_=ogst[o4][:]),
                    reads=[("ogst", o4)], writes=[("ogd", s, 1, c, h)], dma=f"ogst{o4}")

            b1(NT - 1)
            b1(NT - 2)
            for c in range(NT - 1, -1, -1):
                b2(c)
                if c + 1 <= NT - 1:
                    b_og(c + 1)
                if c - 2 >= 0:
                    b1(c - 2)
            b_og(0)

        def load_tables(src_d):
            for q in range(2):
                P.add("pool", lambda e, q=q: e.dma_start(out=TAB[:, q * 4096:(q + 1) * 4096], in_=src_d[:, q * 4096:(q + 1) * 4096]),
                      writes=[("tab", q)], dma=f"tab{q}")

        for s in range(NSEQ):
            if s > 0:
                P.new_epoch()
            P.barrier()
            if 0 in do_layers:
                load_tables(tab0_d)
                P.add("pool", lambda e: e.dma_start(out=Wh_flat, in_=win0_d[0, :, :]), writes=["Wh"], dma="Wh")
                P.add("pool", lambda e: e.memset(Vb[:, :, 256:258], 1.0), writes=["Vones"])
            phase0(s)
            P.barrier()
            if 0 in do_layers:
                for h in range(NH):
                    projection(0, h, s, skip_w=(h == 0))
                    if h == NH - 1:
                        prefetch_wout(0)
                    attention(h, s)
                outproj(0, s)
            if 1 in do_layers:
                P.add("pool", lambda e: e.dma_start(out=Wh[:, :, 0:768],
                                                    in_=win1_d[0, :, :].rearrange("p (c n) -> p c n", n=768)),
                      writes=["Wh"], dma="Wh")
                load_tables(tab1_d)
                for h in range(NH):
                    ret_consts(h)
                    projection(1, h, s, skip_w=(h == 0))
                    if h == NH - 1:
                        prefetch_wout(1)
                    retention(h, s)
                outproj(1, s)
        P.barrier()
        P.add("sp", lambda e: None)

        keys = P.semkeys()
        P.assign()
        sems = {k: E(nc.semaphore("s_" + "_".join(str(x) for x in k))) for k in keys}
        block = E(nc.Block())

        @block.tensor
        def _(e):
            P.emit_engine("pe", e, sems)

        @block.scalar
        def _(e):
            P.emit_engine("act", e, sems)

        @block.vector
        def _(e):
            P.emit_engine("dve", e, sems)

        @block.gpsimd
        def _(e):
            P.emit_engine("pool", e, sems)

        @block.sync
        def _(e):
            P.emit_engine("sp", e, sems)
    return nc, P


def _tables():
    t = np.arange(S, dtype=np.float32)
    inv0 = (10000.0 ** (-np.arange(0, 128, 2, dtype=np.float32) / 128)).astype(np.float32)
    ang0 = (t[:, None] * inv0[None, :]).astype(np.float32)
    inv1 = (1.0 / (10000.0 ** np.linspace(0.0, 1.0, 64, dtype=np.float32))).astype(np.float32)
    ang1 = (t[:, None] * inv1[None, :]).astype(np.float32)

    def mk(ang):
        c = np.cos(ang).astype(np.float32).T
        sn = np.sin(ang).astype(np.float32).T
        cos = np.concatenate([c, c], 0)
        sin = np.concatenate([-sn, sn], 0)
        return np.ascontiguousarray(np.concatenate([cos, sin], 1))
    return mk(ang0), mk(ang1)


def _cmask():
    i = np.arange(128, dtype=np.float32)
    jj = i[:, None]
    ii = i[None, :]
    Mf = np.maximum(ii - jj, 0.0)
    Mb = np.maximum(jj - ii, 0.0)
    mkf = (ii >= jj).astype(np.float32)
    mkb = (jj > ii).astype(np.float32)
    RI1 = np.broadcast_to(ii + 1.0, (128, 128))
    RCI = np.broadcast_to(128.0 - ii, (128, 128))
    cm = np.stack([Mf, Mb, mkf, mkb, RI1, RCI], 1).astype(np.float32)
    cj = np.stack([127.0 - i, i], 1).astype(np.float32)
    return np.ascontiguousarray(cm.reshape(128, 768)), np.ascontiguousarray(cj)


def _rep(v, n=128):
    v = np.asarray(v, dtype=np.float32).reshape(1, -1)
    return np.ascontiguousarray(np.broadcast_to(v, (n, v.shape[1])))


def prep_shared(norm_g, da_w_in, da_lambda_q1, da_lambda_k1, da_lambda_q2, da_lambda_k2, da_subln_g,
                da_w_out, ret_w_in, ret_decay_fwd, ret_decay_bwd, ret_subln_g, ret_w_out, final_norm_g):
    w0 = np.asarray(da_w_in[0], dtype=np.float32)
    w0 = w0.reshape(8, 128, 4, 8, 256)
    w_in0 = np.ascontiguousarray(w0.transpose(3, 1, 0, 2, 4)).reshape(8, 128, 8192)
    w1 = np.asarray(ret_w_in[0], dtype=np.float32)
    perm = np.concatenate([np.arange(0, 128, 2), np.arange(1, 128, 2)])
    q1 = w1[:, 0:1024].reshape(8, 128, 8, 128)[:, :, :, perm]
    k1 = w1[:, 1024:2048].reshape(8, 128, 8, 128)[:, :, :, perm]
    v1 = w1[:, 2048:4096].reshape(8, 128, 8, 256)
    g1w = w1[:, 4096:6144].reshape(8, 128, 8, 256)
    cat = np.concatenate([q1, k1, v1, g1w], axis=3)
    w_in1 = np.ascontiguousarray(cat.transpose(2, 1, 0, 3)).reshape(8, 128, 6144)

    def wout(w):
        w = np.asarray(w, dtype=np.float32).reshape(16, 128, 1024)
        return np.ascontiguousarray(w.transpose(1, 0, 2)).reshape(128, 16384)
    tab0, tab1 = _tables()
    cm, cj = _cmask()
    shared = {
        "w_in0": w_in0, "w_out0": wout(da_w_out[0]), "w_in1": w_in1, "w_out1": wout(ret_w_out[0]),
        "g0": _rep(norm_g[0]), "g1": _rep(norm_g[1]), "gf": _rep(final_norm_g),
        "sub": np.ascontiguousarray(np.concatenate([_rep(da_subln_g[0]), _rep(ret_subln_g[0])], 1)),
        "lam4": np.ascontiguousarray(np.concatenate([_rep(da_lambda_q1[0]), _rep(da_lambda_k1[0]),
                                                     _rep(da_lambda_q2[0]), _rep(da_lambda_k2[0])], 1)),
        "dec": np.ascontiguousarray(np.concatenate([_rep(ret_decay_fwd[0]), _rep(ret_decay_bwd[0])], 1)),
        "tab0": tab0, "tab1": tab1, "cmask": cm, "cj": cj,
    }
    return shared


_CACHE = {}


def kernel(x_prompt, x_sample, norm_g, da_w_in, da_lambda_q1, da_lambda_k1, da_lambda_q2,
           da_lambda_k2, da_subln_g, da_w_out, ret_w_in, ret_decay_fwd, ret_decay_bwd,
           ret_subln_g, ret_w_out, final_norm_g):
    x_prompt = np.asarray(x_prompt, dtype=np.float32)
    x_sample = np.asarray(x_sample, dtype=np.float32)
    shared = prep_shared(norm_g, da_w_in, da_lambda_q1, da_lambda_k1, da_lambda_q2, da_lambda_k2, da_subln_g,
                         da_w_out, ret_w_in, ret_decay_fwd, ret_decay_bwd, ret_subln_g, ret_w_out, final_norm_g)
    xs_all = np.concatenate([x_prompt, x_sample], 0)
    n = 8
    per = xs_all.shape[0] // n
    if "nc" not in _CACHE:
        _CACHE["nc"] = build(NSEQ=per)[0]
    nc = _CACHE["nc"]
    in_maps = []
    for c in range(n):
        m = dict(shared)
        m["x"] = np.ascontiguousarray(xs_all[c * per:(c + 1) * per])
        in_maps.append(m)
    res = run_bass_kernel_spmd(nc, in_maps, core_ids=list(range(n)))
    y = np.concatenate([np.asarray(r["y"], dtype=np.float32) for r in res.results], 0)
    nb = x_prompt.shape[0]
    return (np.ascontiguousarray(y[:nb]), np.ascontiguousarray(y[nb:]))
```
